# Optimizing a Trainium2 kernel written in Bass

```python
import math
import jax
import jax.numpy as jnp
from jax import lax
import numpy as np

D_MODEL = 1024
BATCH = 2
SEQ = 16384
DEPTH = 1
DEC_BATCH = 32
DEC_SEQ = 32
PAST_LEN = 2048

CHUNK = 64
WINDOW = 128
HEAD_DIM = 64
MIX_W = D_MODEL // 2
N_HEADS = MIX_W // HEAD_DIM
N_KV_HEADS = N_HEADS // 4
KV_REP = N_HEADS // N_KV_HEADS
ATTN_W = N_HEADS * HEAD_DIM
KV_W = N_KV_HEADS * HEAD_DIM
SSM_GROUP = 16
SSM_W = MIX_W
SSM_GROUPS = SSM_W // SSM_GROUP
SSM_STATE = 64
MEM_LEN = 256
MEM_HEADS = 4
MEM_HEAD_DIM = MIX_W // MEM_HEADS
MEM_W = MEM_HEADS * MEM_HEAD_DIM
N_BRANCH = 3
IN_COLS = ATTN_W + 2 * KV_W + SSM_W + MEM_W + N_BRANCH * D_MODEL
D_FF = 128 * ((8 * D_MODEL // 3 + 127) // 128)
N_BUCKETS = 32
MAX_DISTANCE = 128
RMS_EPS = 1e-6
NEG_INF = -1e30
DT_MIN = 1e-3
DT_MAX = 1e-1

kernel_name = 'hybrid_streaming_encoder_step'


def rmsnorm(x, g):
    x32 = x.astype(jnp.float32)
    y = x32 * lax.rsqrt(jnp.mean(x32 * x32, axis=-1, keepdims=True) + RMS_EPS)
    return (y * g.astype(jnp.float32)).astype(x.dtype)


def swiglu(h, w_in, w_out):
    g, u = jnp.split(h @ w_in, 2, axis=-1)
    return (jax.nn.silu(g) * u) @ w_out


def t5_bucket(rel):
    half = N_BUCKETS // 2
    max_exact = half // 2
    ret = (rel > 0).astype(np.int32) * half
    n = np.abs(rel)
    large = max_exact + (np.log(np.maximum(n, 1) / max_exact) / math.log(MAX_DISTANCE / max_exact) * (half - max_exact)).astype(np.int32)
    large = np.minimum(large, half - 1)
    return ret + np.where(n < max_exact, n, large)


def band_bias(rel_table, n_q, n_back, n_k):
    i = np.arange(n_q)[:, None]
    j = np.arange(n_k)[None, :]
    bucket = t5_bucket((j - n_back) - i)
    b = rel_table[bucket].astype(jnp.float32)
    return jnp.transpose(b, (2, 0, 1)).reshape(N_KV_HEADS, KV_REP, n_q, n_k)


def sink_attention(q, k, v, bias, sink, key_valid=None):
    logits = jnp.einsum('...qgrd,...kgd->...grqk', q, k, preferred_element_type=jnp.float32) * (HEAD_DIM ** -0.5) + bias
    if key_valid is not None:
        logits = jnp.where(key_valid, logits, NEG_INF)
    sink_l = sink.astype(jnp.float32).reshape(N_KV_HEADS, KV_REP)[:, :, None, None]
    m = jnp.maximum(jnp.max(logits, axis=-1, keepdims=True), sink_l)
    e = jnp.exp(logits - m)
    p = e / (jnp.sum(e, axis=-1, keepdims=True) + jnp.exp(sink_l - m))
    return jnp.einsum('...grqk,...kgd->...qgrd', p.astype(v.dtype), v)


def swa_prompt(q, k, v, rel_table, sink):
    b, L = q.shape[0], q.shape[1]
    nc = L // CHUNK
    nb = WINDOW // CHUNK
    band = (nb + 1) * CHUNK
    qc = q.reshape(b, nc, CHUNK, N_KV_HEADS, KV_REP, HEAD_DIM)
    pad = ((0, 0), (WINDOW, 0), (0, 0), (0, 0))
    kc = jnp.pad(k, pad).reshape(b, nc + nb, CHUNK, N_KV_HEADS, HEAD_DIM)
    vc = jnp.pad(v, pad).reshape(b, nc + nb, CHUNK, N_KV_HEADS, HEAD_DIM)
    kb = jnp.concatenate([kc[:, s:s + nc] for s in range(nb + 1)], axis=2)
    vb = jnp.concatenate([vc[:, s:s + nc] for s in range(nb + 1)], axis=2)
    key_pos = np.arange(nc)[:, None] * CHUNK - WINDOW + np.arange(band)[None, :]
    valid = jnp.asarray(key_pos >= 0)[:, None, None, None, :]
    bias = band_bias(rel_table, CHUNK, WINDOW, band)
    out = sink_attention(qc, kb, vb, bias, sink, valid)
    return out.reshape(b, L, ATTN_W)


def swa_sample(q, k, v, k_cache, v_cache, rel_table, sink):
    b, s = q.shape[0], q.shape[1]
    n_back = k_cache.shape[1]
    kk = jnp.concatenate([k_cache.astype(k.dtype), k], axis=1)
    vv = jnp.concatenate([v_cache.astype(v.dtype), v], axis=1)
    bias = band_bias(rel_table, s, n_back, n_back + s)
    out = sink_attention(q, kk, vv, bias, sink)
    return out.reshape(b, s, ATTN_W)


def ssm_discretise(lam_re, lam_im, log_dt, b_re, b_im):
    f32 = jnp.float32
    dt = jnp.exp(log_dt.astype(f32))[:, None]
    lr, li = lam_re.astype(f32), lam_im.astype(f32)
    mag = jnp.exp(lr * dt)
    a_re, a_im = mag * jnp.cos(li * dt), mag * jnp.sin(li * dt)
    den = lr * lr + li * li
    coef_re = ((a_re - 1.0) * lr + a_im * li) / den
    coef_im = (a_im * lr - (a_re - 1.0) * li) / den
    br, bi = b_re.astype(f32), b_im.astype(f32)
    cr, ci = coef_re[..., None], coef_im[..., None]
    return a_re, a_im, cr * br - ci * bi, cr * bi + ci * br


def complex_affine_combine(e1, e2):
    a1r, a1i, b1r, b1i = e1
    a2r, a2i, b2r, b2i = e2
    return (a2r * a1r - a2i * a1i, a2r * a1i + a2i * a1r,
            a2r * b1r - a2i * b1i + b2r, a2r * b1i + a2i * b1r + b2i)


def ssm_branch(u, state, w):
    b, L, _ = u.shape
    f32 = jnp.float32
    ug = u.reshape(b, L, SSM_GROUPS, SSM_GROUP).astype(f32)
    a_re, a_im, bb_re, bb_im = ssm_discretise(w['ssm_lambda_re'], w['ssm_lambda_im'], w['ssm_log_dt'], w['ssm_b_re'], w['ssm_b_im'])
    bu_re = jnp.einsum('blgc,gpc->blgp', ug, bb_re)
    bu_im = jnp.einsum('blgc,gpc->blgp', ug, bb_im)
    shape = bu_re.shape
    elems = (jnp.broadcast_to(a_re, shape), jnp.broadcast_to(a_im, shape), bu_re, bu_im)
    ac_re, ac_im, s_re, s_im = lax.associative_scan(complex_affine_combine, elems, axis=1)
    if state is not None:
        s0_re = state[0].astype(f32)[:, None]
        s0_im = state[1].astype(f32)[:, None]
        s_re, s_im = (s_re + ac_re * s0_re - ac_im * s0_im, s_im + ac_re * s0_im + ac_im * s0_re)
    c_re, c_im = w['ssm_c_re'].astype(f32), w['ssm_c_im'].astype(f32)
    y = (jnp.einsum('blgp,gcp->blgc', s_re, c_re) - jnp.einsum('blgp,gcp->blgc', s_im, c_im)
         + w['ssm_d'].astype(f32) * ug)
    y = y.reshape(b, L, SSM_W).astype(u.dtype)
    ya, yb = jnp.split(y @ w['w_ssm_glu'], 2, axis=-1)
    return ya * jax.nn.sigmoid(yb), s_re[:, -1], s_im[:, -1]


def memory_kv(mem, g, w_kv):
    mk, mv = jnp.split(rmsnorm(mem, g) @ w_kv, 2, axis=-1)
    shape = mem.shape[:2] + (MEM_HEADS, MEM_HEAD_DIM)
    return mk.reshape(shape), mv.reshape(shape)


def cross_attention(q, mk, mv):
    logits = jnp.einsum('blhd,bmhd->bhlm', q, mk, preferred_element_type=jnp.float32) * (MEM_HEAD_DIM ** -0.5)
    p = jax.nn.softmax(logits, axis=-1).astype(mv.dtype)
    return jnp.einsum('bhlm,bmhd->blhd', p, mv)


def layer(x, mem_k, mem_v, swa_cache, ssm_state, rel_table, w):
    b, L, _ = x.shape
    x = x + 0.5 * rmsnorm(swiglu(rmsnorm(x, w['ff1_pre_g']), w['w_ff1_in'], w['w_ff1_out']), w['ff1_post_g'])
    h = rmsnorm(x, w['mix_pre_g'])
    cuts = np.cumsum([ATTN_W, KV_W, KV_W, SSM_W, MEM_W]).tolist()
    q, k, v, u, qm, gate_logits = jnp.split(h @ w['w_in'], cuts, axis=-1)
    q = q.reshape(b, L, N_KV_HEADS, KV_REP, HEAD_DIM)
    k = k.reshape(b, L, N_KV_HEADS, HEAD_DIM)
    v = v.reshape(b, L, N_KV_HEADS, HEAD_DIM)
    if swa_cache is None:
        attn = swa_prompt(q, k, v, rel_table, w['attn_sink'])
    else:
        attn = swa_sample(q, k, v, swa_cache[0], swa_cache[1], rel_table, w['attn_sink'])
    ssm_out, s_re, s_im = ssm_branch(u, ssm_state, w)
    mem_out = cross_attention(qm.reshape(b, L, MEM_HEADS, MEM_HEAD_DIM), mem_k.astype(x.dtype), mem_v.astype(x.dtype)).reshape(b, L, MEM_W)
    gates = jax.nn.sigmoid(gate_logits).reshape(b, L, N_BRANCH, D_MODEL)
    merged = (gates[:, :, 0] * (attn @ w['w_attn_br']) + gates[:, :, 1] * ssm_out
              + gates[:, :, 2] * (mem_out @ w['w_mem_br']))
    x = x + rmsnorm(merged @ w['w_out'], w['mix_post_g'])
    x = x + 0.5 * rmsnorm(swiglu(rmsnorm(x, w['ff2_pre_g']), w['w_ff2_in'], w['w_ff2_out']), w['ff2_post_g'])
    return x, k, v, s_re, s_im


def setup_inputs(seed: int = 0) -> dict:
    key = jax.random.key(seed)
    ks = iter(jax.random.split(key, 48))
    f32 = jnp.float32
    nrm = lambda shape, scale: scale * jax.random.normal(next(ks), shape, f32)
    gain = lambda shape: 1.0 + 0.02 * jax.random.normal(next(ks), shape, f32)
    n_back = min(WINDOW, PAST_LEN)
    lam_im_base = jnp.pi * jnp.arange(SSM_STATE, dtype=f32)
    return {
        'x_prompt': nrm((BATCH, SEQ, D_MODEL), 1.0),
        'x_sample': nrm((DEC_BATCH, DEC_SEQ, D_MODEL), 1.0),
        'cache_swa_k': nrm((DEPTH, DEC_BATCH, n_back, N_KV_HEADS, HEAD_DIM), 1.0),
        'cache_swa_v': nrm((DEPTH, DEC_BATCH, n_back, N_KV_HEADS, HEAD_DIM), 1.0),
        'cache_mem_k': nrm((DEPTH, DEC_BATCH, MEM_LEN, MEM_HEADS, MEM_HEAD_DIM), 1.0),
        'cache_mem_v': nrm((DEPTH, DEC_BATCH, MEM_LEN, MEM_HEADS, MEM_HEAD_DIM), 1.0),
        'state_ssm_re': nrm((DEPTH, DEC_BATCH, SSM_GROUPS, SSM_STATE), 0.1),
        'state_ssm_im': nrm((DEPTH, DEC_BATCH, SSM_GROUPS, SSM_STATE), 0.1),
        'mem_prompt': nrm((BATCH, MEM_LEN, D_MODEL), 1.0),
        'rel_bias_table': nrm((N_BUCKETS, N_HEADS), 0.2),
        'ff1_pre_g': gain((DEPTH, D_MODEL)),
        'ff1_post_g': gain((DEPTH, D_MODEL)),
        'w_ff1_in': nrm((DEPTH, D_MODEL, 2 * D_FF), D_MODEL ** -0.5),
        'w_ff1_out': nrm((DEPTH, D_FF, D_MODEL), D_FF ** -0.5),
        'mix_pre_g': gain((DEPTH, D_MODEL)),
        'mix_post_g': gain((DEPTH, D_MODEL)),
        'w_in': nrm((DEPTH, D_MODEL, IN_COLS), D_MODEL ** -0.5),
        'mem_norm_g': gain((DEPTH, D_MODEL)),
        'w_mem_kv': nrm((DEPTH, D_MODEL, 2 * MEM_W), D_MODEL ** -0.5),
        'attn_sink': nrm((DEPTH, N_HEADS), 0.5),
        'ssm_lambda_re': -0.5 + nrm((DEPTH, SSM_GROUPS, SSM_STATE), 0.01),
        'ssm_lambda_im': lam_im_base + nrm((DEPTH, SSM_GROUPS, SSM_STATE), 0.01),
        'ssm_log_dt': jax.random.uniform(next(ks), (DEPTH, SSM_GROUPS), f32, math.log(DT_MIN), math.log(DT_MAX)),
        'ssm_b_re': nrm((DEPTH, SSM_GROUPS, SSM_STATE, SSM_GROUP), (2 * SSM_GROUP) ** -0.5),
        'ssm_b_im': nrm((DEPTH, SSM_GROUPS, SSM_STATE, SSM_GROUP), (2 * SSM_GROUP) ** -0.5),
        'ssm_c_re': nrm((DEPTH, SSM_GROUPS, SSM_GROUP, SSM_STATE), SSM_STATE ** -0.5),
        'ssm_c_im': nrm((DEPTH, SSM_GROUPS, SSM_GROUP, SSM_STATE), SSM_STATE ** -0.5),
        'ssm_d': nrm((DEPTH, SSM_GROUPS, SSM_GROUP), 1.0),
        'w_ssm_glu': nrm((DEPTH, SSM_W, 2 * D_MODEL), SSM_W ** -0.5),
        'w_attn_br': nrm((DEPTH, ATTN_W, D_MODEL), ATTN_W ** -0.5),
        'w_mem_br': nrm((DEPTH, MEM_W, D_MODEL), MEM_W ** -0.5),
        'w_out': nrm((DEPTH, D_MODEL, D_MODEL), D_MODEL ** -0.5),
        'ff2_pre_g': gain((DEPTH, D_MODEL)),
        'ff2_post_g': gain((DEPTH, D_MODEL)),
        'w_ff2_in': nrm((DEPTH, D_MODEL, 2 * D_FF), D_MODEL ** -0.5),
        'w_ff2_out': nrm((DEPTH, D_FF, D_MODEL), D_FF ** -0.5),
    }


def reference(x_prompt, x_sample, cache_swa_k, cache_swa_v, cache_mem_k, cache_mem_v,
              state_ssm_re, state_ssm_im, mem_prompt, rel_bias_table,
              ff1_pre_g, ff1_post_g, w_ff1_in, w_ff1_out, mix_pre_g, mix_post_g, w_in,
              mem_norm_g, w_mem_kv, attn_sink, ssm_lambda_re, ssm_lambda_im, ssm_log_dt,
              ssm_b_re, ssm_b_im, ssm_c_re, ssm_c_im, ssm_d, w_ssm_glu, w_attn_br, w_mem_br,
              w_out, ff2_pre_g, ff2_post_g, w_ff2_in, w_ff2_out):
    yp, ys = x_prompt, x_sample
    pk_l, pv_l, pmk_l, pmv_l, pre_l, pim_l = [], [], [], [], [], []
    sk_l, sv_l, sre_l, sim_l = [], [], [], []
    for l in range(DEPTH):
        w = dict(ff1_pre_g=ff1_pre_g[l], ff1_post_g=ff1_post_g[l], w_ff1_in=w_ff1_in[l], w_ff1_out=w_ff1_out[l],
                 mix_pre_g=mix_pre_g[l], mix_post_g=mix_post_g[l], w_in=w_in[l], attn_sink=attn_sink[l],
                 ssm_lambda_re=ssm_lambda_re[l], ssm_lambda_im=ssm_lambda_im[l], ssm_log_dt=ssm_log_dt[l],
                 ssm_b_re=ssm_b_re[l], ssm_b_im=ssm_b_im[l], ssm_c_re=ssm_c_re[l], ssm_c_im=ssm_c_im[l],
                 ssm_d=ssm_d[l], w_ssm_glu=w_ssm_glu[l], w_attn_br=w_attn_br[l], w_mem_br=w_mem_br[l],
                 w_out=w_out[l], ff2_pre_g=ff2_pre_g[l], ff2_post_g=ff2_post_g[l],
                 w_ff2_in=w_ff2_in[l], w_ff2_out=w_ff2_out[l])
        mk, mv = memory_kv(mem_prompt, mem_norm_g[l], w_mem_kv[l])
        yp, pk, pv, pre, pim = layer(yp, mk, mv, None, None, rel_bias_table, w)
        ys, sk, sv, sre, sim = layer(ys, cache_mem_k[l], cache_mem_v[l], (cache_swa_k[l], cache_swa_v[l]),
                                     (state_ssm_re[l], state_ssm_im[l]), rel_bias_table, w)
        n_keep = min(WINDOW, pk.shape[1])
        pk_l.append(pk[:, -n_keep:])
        pv_l.append(pv[:, -n_keep:])
        pmk_l.append(mk)
        pmv_l.append(mv)
        pre_l.append(pre)
        pim_l.append(pim)
        sk_l.append(sk)
        sv_l.append(sv)
        sre_l.append(sre)
        sim_l.append(sim)
    return (yp, ys, jnp.stack(pk_l), jnp.stack(pv_l), jnp.stack(pmk_l), jnp.stack(pmv_l),
            jnp.stack(pre_l), jnp.stack(pim_l), jnp.stack(sk_l), jnp.stack(sv_l),
            jnp.stack(sre_l), jnp.stack(sim_l))
```

```python
import contextlib
import math
import numpy as np
import concourse.bass as bass
import concourse.mybir as mybir
from concourse.bass_utils import run_bass_kernel_spmd

F32 = mybir.dt.float32
BF16 = mybir.dt.bfloat16
AF = mybir.ActivationFunctionType
ALU = mybir.AluOpType
NCORES = 8
D = 1024
DFF = 2816
SEG = 4096
NLIGHT = 24
TT = 512

STREAMS = ("tensor", "scalar", "vector", "gpsimd", "sync")
DEEP = ("scalar", "vector", "gpsimd")


def _key(x):
    if isinstance(x, (str, tuple)):
        return x
    t = getattr(x, "tensor", None)
    return t.name if t is not None else x.name


class Op:
    __slots__ = ("stream", "fn", "reads", "writes", "dma", "group", "deps", "inc", "ticket", "waits", "idx")

    def __init__(self, stream, fn, reads, writes, dma, group):
        self.stream = stream
        self.fn = fn
        self.reads = [_key(r) for r in reads]
        self.writes = [_key(w) for w in writes]
        self.dma = dma
        self.group = group
        self.deps = []
        self.inc = dma
        self.ticket = 0
        self.waits = []


class Prog:
    def __init__(self, nc):
        self.nc = nc
        self.ops = []
        self.stack = contextlib.ExitStack()

    def sb(self, name, shape, dtype):
        return self.stack.enter_context(self.nc.sbuf_tensor("s_" + name, list(shape), dtype))

    def ps(self, name, shape, dtype):
        return self.stack.enter_context(self.nc.psum_tensor("p_" + name, list(shape), dtype))

    def add(self, stream, fn, r=(), w=()):
        op = Op(stream, fn, r, w, False, None)
        self.ops.append(op)
        return op

    def dma(self, stream, out, in_, group=None, r=None, w=None, **kw):
        rr = [in_] if r is None else r
        ww = [out] if w is None else w
        g = group if group is not None else _key(ww[0])
        op = Op(stream, lambda e: e.dma_start(out=out, in_=in_, **kw), rr, ww, True, g)
        self.ops.append(op)
        return op

    def barrier_all(self, stream, keys):
        op = Op(stream, None, keys, [], False, None)
        self.ops.append(op)
        return op

    def build(self):
        nc = self.nc
        writers = {}
        readers = {}
        for i, op in enumerate(self.ops):
            op.idx = i
            deps = {}
            for k in op.reads:
                for d in writers.get(k, {}).values():
                    deps[d.idx] = (d, True)
            for k in op.writes:
                for d in writers.get(k, {}).values():
                    deps.setdefault(d.idx, (d, False))
                for d in readers.get(k, {}).values():
                    deps.setdefault(d.idx, (d, False))
            for d, raw in deps.values():
                if d is op:
                    continue
                if (not d.dma) and (not op.dma) and d.stream == op.stream:
                    if not (raw and op.stream in DEEP):
                        continue
                op.deps.append(d)
                d.inc = True
            wk = ("dma", op.group) if op.dma else op.stream
            for k in op.reads:
                readers.setdefault(k, {})[wk] = op
            for k in op.writes:
                writers.setdefault(k, {})[wk] = op
                readers[k] = {}
        cnt = {}
        for op in self.ops:
            if op.fn is None:
                continue
            if op.dma:
                sk = ("dma", op.group)
                cnt[sk] = cnt.get(sk, 0) + 16
                op.ticket = cnt[sk]
            elif op.inc:
                cnt[op.stream] = cnt.get(op.stream, 0) + 1
                op.ticket = cnt[op.stream]
        sems = {}
        for sk in cnt:
            sems[sk] = self.stack.enter_context(nc.semaphore("sm%d" % len(sems)))
        seen = {s: {} for s in STREAMS}
        for op in self.ops:
            need = {}
            for d in op.deps:
                sk = ("dma", d.group) if d.dma else d.stream
                need[sk] = max(need.get(sk, 0), d.ticket)
            for sk, v in need.items():
                if seen[op.stream].get(sk, 0) >= v:
                    continue
                seen[op.stream][sk] = v
                op.waits.append((sk, v))
        per = {s: [o for o in self.ops if o.stream == s] for s in STREAMS}
        self.nsem = len(sems)

        def runner(s):
            def f(e):
                for op in per[s]:
                    for sk, v in op.waits:
                        e.wait_ge(sems[sk], v)
                    if op.fn is None:
                        continue
                    ins = op.fn(e)
                    if op.dma:
                        ins.then_inc(sems[("dma", op.group)], 16)
                    elif op.inc:
                        ins.then_inc(sems[op.stream], 1)
            return f

        with nc.Block() as block:
            block.tensor(runner("tensor"))
            block.scalar(runner("scalar"))
            block.vector(runner("vector"))
            block.gpsimd(runner("gpsimd"))
            block.sync(runner("sync"))
        self.stack.close()


def mkap(t, offset, pat):
    return bass.AP(t, offset, [list(p) for p in pat])


def bcl(ap, n):
    return mkap(ap.tensor, ap.offset, [list(p) for p in ap.ap] + [[0, n]])


def weight_blocks():
    blks = []
    for pre in ("ff1", "ff2"):
        for hb in range(11):
            blks.append((pre + "_in%d" % hb, "w_" + pre + "_in", 1024, [(hb * 256, 256), (DFF + hb * 256, 256)]))
        for m in range(8):
            blks.append((pre + "_out%d" % m, "w_" + pre + "_out", DFF, [(m * 128, 128)]))
    blks.append(("in_q", "w_in", 1024, [(0, 512)]))
    blks.append(("in_kv", "w_in", 1024, [(512, 256)]))
    blks.append(("in_u", "w_in", 1024, [(768, 512)]))
    blks.append(("in_qm", "w_in", 1024, [(1280, 512)]))
    for i in range(6):
        blks.append(("in_g%d" % i, "w_in", 1024, [(1792 + 512 * i, 512)]))
    for i in range(4):
        blks.append(("glu%d" % i, "w_ssm_glu", 512, [(512 * i, 512)]))
    blks.append(("abr", "w_attn_br", 512, [(0, 1024)]))
    blks.append(("mbr", "w_mem_br", 512, [(0, 1024)]))
    for i in range(2):
        blks.append(("out%d" % i, "w_out", 1024, [(512 * i, 512)]))
    for i in range(2):
        blks.append(("mkv%d" % i, "w_mem_kv", 1024, [(512 * i, 512)]))
    return blks


WSHAPES = {"w_ff1_in": (1024, 2 * DFF), "w_ff1_out": (DFF, 1024), "w_ff2_in": (1024, 2 * DFF), "w_ff2_out": (DFF, 1024),
           "w_in": (1024, 4864), "w_ssm_glu": (512, 2048), "w_attn_br": (512, 1024), "w_mem_br": (512, 1024),
           "w_out": (1024, 1024), "w_mem_kv": (1024, 1024)}


def build_nc(nlight=NLIGHT, nseg=8, dbg=False):
    nc = bass.Bass("TRN2", target_bir_lowering=False)
    P = Prog(nc)

    def din(name, shape):
        return nc.dram_tensor(name, list(shape), F32, kind="ExternalInput").ap()

    def dout(name, shape):
        return nc.dram_tensor(name, list(shape), F32, kind="ExternalOutput").ap()

    xl = din("xl", [max(nlight, 1) * TT, D])
    xf = din("xf", [256 + nseg * TT, D])
    valid_d = din("valid", [128, 1])
    csk_d = din("csk", [4, 128, 128])
    csv_d = din("csv", [4, 128, 128])
    cmk_d = din("cmk", [4, 256, 512])
    cmv_d = din("cmv", [4, 256, 512])
    sre_d = din("sre", [128, 64])
    sim_d = din("sim", [128, 64])
    mem_d = din("mem", [256, D])
    biasP_d = din("biasP", [8, 128, 256])
    biasSc_d = din("biasSc", [8, 128, 32])
    biasSn_d = din("biasSn", [8, 128, 128])
    ident_d = din("ident", [128, 128])
    par_d = din("par", [128, 8])
    gains_d = {n: din(n, [D]) for n in ("ff1_pre_g", "ff1_post_g", "mix_pre_g", "mix_post_g", "mem_norm_g", "ff2_pre_g", "ff2_post_g")}
    sink_d = din("attn_sink", [8])
    lam_re_d = din("ssm_lambda_re", [32, 64])
    lam_im_d = din("ssm_lambda_im", [32, 64])
    logdt_d = din("ssm_log_dt", [32])
    bre_d = din("ssm_b_re", [32, 64, 16])
    bim_d = din("ssm_b_im", [32, 64, 16])
    cre_d = din("ssm_c_re", [32, 16, 64])
    cim_d = din("ssm_c_im", [32, 16, 64])
    dd_d = din("ssm_d", [512])
    wd = {n: din(n, s) for n, s in WSHAPES.items()}

    yf = dout("yf", [256 + nseg * TT, D])
    kp_o = dout("kp", [128, 128])
    vp_o = dout("vp", [128, 128])
    mkp_o = dout("mkp", [256, 512])
    mvp_o = dout("mvp", [256, 512])
    ssp_o = dout("ssp", [32, 128])
    ks_o = dout("ks", [128, 128])
    vs_o = dout("vs", [128, 128])
    sss_o = dout("sss", [128, 128])
    outs = ["yf", "kp", "vp", "mkp", "mvp", "ssp", "ks", "vs", "sss"]
    import os
    DBG = os.environ.get("KDBG") is not None
    if DBG:
        dbg_o = dout("dbg", [128, 16384])
        outs.append("dbg")
    dbgpos = [0]

    def dump(ap2d, n):
        if not DBG:
            return
        P.dma("gpsimd", dbg_o[:, dbgpos[0]:dbgpos[0] + n], ap2d, group="dbg")
        dbgpos[0] += n
        dbgpos[0] = (dbgpos[0] + 127) // 128 * 128

    blks = weight_blocks()
    scr = {}
    binfo = {}
    for (bn, wn, K, cols) in blks:
        kc = K // 128
        w = sum(c[1] for c in cols)
        scr[bn] = nc.dram_tensor("scr_" + bn, [128, kc * w], BF16, kind="Internal").ap()
        binfo[bn] = (kc, w)
        off = 0
        for (c0, cw) in cols:
            src = wd[wn][:, c0:c0 + cw].rearrange("(kc p) w -> p kc w", p=128)
            dst = scr[bn].rearrange("p (kc w) -> p kc w", kc=kc)[:, :, off:off + cw]
            P.dma("gpsimd", dst, src, group="cv_" + bn, w=[scr[bn]])
            off += cw

    NSLOT = 3
    wslots = [P.sb("wslot%d" % i, [128, 4096], BF16) for i in range(NSLOT)]
    wctr = [0]

    def wload(bn):
        kc, w = binfo[bn]
        s = wslots[wctr[0] % NSLOT]
        wctr[0] += 1
        P.dma("sync", s[:, 0:kc * w], scr[bn], group=s.name)
        return s[:, 0:kc * w].rearrange("p (kc w) -> p kc w", kc=kc)

    banks = [P.ps("bank%d" % i, [128, 512], F32) for i in range(4)]
    bctr = [0]

    def bank():
        b = banks[bctr[0] % 4]
        bctr[0] += 1
        return b
    pSb = [P.ps("pS%d" % i, [128, 512], F32) for i in range(2)]
    pWb = [P.ps("pW%d" % i, [128, 512], F32) for i in range(2)]

    xT = P.sb("xT", [128, 8, TT], F32)
    hT = P.sb("hT", [128, 8, TT], BF16)
    arena = P.sb("arena", [128, 22 * TT], BF16)
    hid = arena[:, :].rearrange("p (c t) -> p c t", t=TT)
    qT = arena[:, 0:4 * TT].rearrange("p (c t) -> p c t", t=TT)
    uT = arena[:, 4 * TT:8 * TT].rearrange("p (c t) -> p c t", t=TT)
    qmT = arena[:, 16 * TT:20 * TT].rearrange("p (c t) -> p c t", t=TT)
    yst = P.sb("yst", [128, 8, TT], F32)
    ytok = yst[:, :, :].rearrange("p c t -> p (c t)").rearrange("p (b f) -> p b f", f=D)
    sqb = [P.sb("sqb%d" % i, [128, TT], BF16) for i in range(2)]
    rstd = P.sb("rstd", [128, TT], F32)
    tmpf = P.sb("tmpf", [128, TT], F32)
    tmpb = [P.sb("tmpb%d" % i, [128, TT], BF16) for i in range(2)]
    attnT = P.sb("attnT", [128, 4, TT], BF16)
    memT = P.sb("memT", [128, 4, TT], BF16)
    ysT = P.sb("ysT", [128, 4, TT], BF16)
    mergedT = P.sb("mergedT", [128, 8, TT], BF16)
    kTz = [[P.sb("kTz%d%d" % (i, j), [128, 128 + TT], BF16) for j in range(2)] for i in range(2)]
    vtok = P.sb("vtok", [128, 5, 128], BF16)
    kvf = P.sb("kvf", [128, 256], F32)
    Eb = [P.sb("Eb%d" % i, [128, 2, TT], BF16) for i in range(2)]
    ident = P.sb("ident", [128, 128], F32)
    identb = P.sb("identb", [128, 128], BF16)
    onesb = P.sb("onesb", [128, 128], BF16)
    onesv = P.sb("onesv", [128, 64], BF16)
    zer = P.sb("zer", [128, TT], BF16)
    valid = P.sb("validt", [128, 1], F32)
    par = P.sb("part", [128, 8], F32)
    gains = {n: P.sb("g_" + n, [128, 8], F32) for n in gains_d}
    hgains = {n: P.sb("hg_" + n, [128, 8], F32) for n in ("ff1_post_g", "ff2_post_g")}
    esink = P.sb("esink", [128, 4], F32)
    biasP = P.sb("biasP", [128, 8, 256], BF16)
    biasSc = P.sb("biasSc", [128, 8, 32], BF16)
    biasSn = P.sb("biasSn", [128, 8, 128], BF16)
    bstage = yst[:, 0:4, :].rearrange("p a t -> p (a t)").rearrange("p (h n) -> p h n", n=256)
    mkT = P.sb("mkT", [128, 4, 256], BF16)
    mvt = P.sb("mvt", [128, 2, 512], BF16)
    cstage = yst[:, 0:2, :]
    csb = P.sb("csb", [128, 2, 128], BF16)
    P_kpad_holder = [P.sb("kpad", [128, 2, 128], F32)]
    WS = yst[:, 6:8, :].rearrange("p a t -> p (a t)").rearrange("p (q v m) -> p q v m", q=4, v=2)
    WSg = P.sb("WSg", [128, 32, 2, 128], BF16)
    CWp = P.sb("CWp", [128, 2, 512], BF16)
    COS = P.sb("COS", [128, 32, 65], F32)
    SIN = P.sb("SIN", [128, 32, 65], F32)
    RT = P.sb("RT", [128, 32, 65], F32)
    Wb = P.sb("Wb", [128, 16, 65], F32)
    Wpb = P.sb("Wpb", [128, 16, 65], F32)
    t1 = P.sb("t1", [128, 16, 65], F32)
    t2 = P.sb("t2", [128, 16, 65], F32)
    sbf = P.sb("sbf", [128, 16, 65], BF16)
    Dcol = P.sb("Dcol", [128, 4], F32)
    car = P.sb("car", [128, 2, 32], F32)
    cars = P.sb("cars", [128, 2, 4, 32], F32)
    sm = P.sb("sm", [128, 16, 32], F32)
    wnT = [P.sb("wn%d" % i, [128, 16], F32) for i in range(2)]
    wswT = [P.sb("wsw%d" % i, [128, 16], F32) for i in range(2)]
    SINS = P.sb("SINS", [128, 32], F32)

    V = "vector"
    A = "scalar"
    G = "gpsimd"
    T = "tensor"

    def mm(out, lhsT, rhs, start=True, stop=True, tp=None):
        if tp is None:
            P.add(T, lambda e: e.matmul(out, lhsT=lhsT, rhs=rhs, start=start, stop=stop), r=[lhsT, rhs], w=[out])
        else:
            P.add(T, lambda e: e.matmul(out, lhsT=lhsT, rhs=rhs, start=start, stop=stop, tile_position=tp), r=[lhsT, rhs], w=[out])

    def tr(out, in_, idn):
        P.add(T, lambda e: e.transpose(out, in_, idn), r=[in_, idn], w=[out])

    def tt(out, a, b, op, eng=V):
        P.add(eng, lambda e: e.tensor_tensor(out, a, b, op), r=[a, b], w=[out])

    def ts(out, a, s1, op0, s2=None, op1=None, eng=V):
        rr = [a] + [s for s in (s1, s2) if not isinstance(s, (int, float, type(None)))]
        if op1 is None:
            P.add(eng, lambda e: e.tensor_scalar(out, a, s1, None, op0), r=rr, w=[out])
        else:
            P.add(eng, lambda e: e.tensor_scalar(out, a, s1, s2, op0, op1), r=rr, w=[out])

    def stt(out, a, s, b, op0, op1, eng=V):
        rr = [a, b] + ([] if isinstance(s, (int, float)) else [s])
        P.add(eng, lambda e: e.scalar_tensor_tensor(out, a, s, b, op0, op1), r=rr, w=[out])

    def cp(out, a, eng=V):
        P.add(eng, lambda e: e.tensor_copy(out, a), r=[a], w=[out])

    def act(out, a, func, scale=1.0, bias=None, eng=A):
        rr = [a] + ([] if bias is None or isinstance(bias, (int, float)) else [bias])
        if bias is None:
            P.add(eng, lambda e: e.activation(out=out, in_=a, func=func, scale=scale), r=rr, w=[out])
        else:
            P.add(eng, lambda e: e.activation(out=out, in_=a, func=func, scale=scale, bias=bias), r=rr, w=[out])

    def recip(out, a):
        P.add(V, lambda e: e.reciprocal(out, a), r=[a], w=[out])

    def mset(ap, v, eng=V):
        P.add(eng, lambda e: e.memset(ap, v), w=[ap])

    P.dma("sync", ident[:], ident_d)
    P.dma("sync", valid[:], valid_d)
    P.dma("sync", par[:], par_d)
    for n in gains_d:
        P.dma("sync", gains[n][:], gains_d[n].rearrange("(c p) -> p c", p=128), allow_slow_non_contiguous=True)
    for n in hgains:
        ts(hgains[n][:], gains[n][:], 0.5, ALU.mult)
    cp(identb[:], ident[:])
    mset(onesb[:], 1.0)
    mset(zer[:], 0.0)
    mset(tmpf[:, 0:64], 1.0)
    ts(onesv[:], tmpf[:, 0:64], valid[:, 0:1], ALU.mult)
    P.dma("sync", esink[0:64, :], mkap(sink_d.tensor, 0, [[0, 64], [2, 4]]), allow_slow_non_contiguous=True)
    P.dma("sync", esink[64:128, :], mkap(sink_d.tensor, 1, [[0, 64], [2, 4]]), allow_slow_non_contiguous=True)
    act(esink[:], esink[:], AF.Exp)
    P.dma("sync", bstage[:, :, :], biasP_d.rearrange("h k n -> k h n"))
    ts(biasP[:], bstage[:, :, :], 8.0, ALU.mult)
    P.dma("sync", bstage[:, :, 0:32], biasSc_d.rearrange("h k n -> k h n"))
    ts(biasSc[:], bstage[:, :, 0:32], 8.0, ALU.mult)
    P.dma("sync", bstage[:, :, 0:128], biasSn_d.rearrange("h k n -> k h n"))
    ts(biasSn[:], bstage[:, :, 0:128], 8.0, ALU.mult)
    for a in range(2):
        for b2 in range(2):
            mset(kTz[a][b2][:], 0.0)
    mset(P_kpad_holder[0][:], 0.0)

    lr = sm[:, 0, :]
    li = sm[:, 1, :]
    dt = sm[:, 2, :]
    for i, (dst, src) in enumerate(((lr, lam_re_d), (li, lam_im_d))):
        P.dma("sync", kvf[0:32, 0:64], src)
        P.dma("sync", kvf[0:32, 64:128], src)
        pb = bank()
        tr(pb[:, 0:32], kvf[0:32, 0:128], ident[0:32, 0:32])
        cp(dst, pb[:, 0:32])
    P.dma("sync", dt, mkap(logdt_d.tensor, 0, [[0, 128], [1, 32]]))
    act(dt, dt, AF.Exp)
    mag = sm[:, 3, :]
    cc = sm[:, 4, :]
    ss = sm[:, 5, :]
    q1 = sm[:, 6, :]
    q2 = sm[:, 7, :]
    are = sm[:, 8, :]
    aim = sm[:, 9, :]
    cre = sm[:, 10, :]
    cim = sm[:, 11, :]
    q3 = sm[:, 12, :]
    tt(q1, lr, dt, ALU.mult)
    act(mag, q1, AF.Exp)
    tt(q2, li, dt, ALU.mult)
    zz = sm[:, 13, :]
    z2 = sm[:, 14, :]
    vv = sm[:, 15, :]
    ts(zz, q2, 1.0 / 32, ALU.mult)
    tt(z2, zz, zz, ALU.mult)
    mset(q1, 1.0 / 362880)
    for cf in (-1.0 / 5040, 1.0 / 120, -1.0 / 6, 1.0):
        tt(q1, q1, z2, ALU.mult)
        ts(q1, q1, cf, ALU.add)
    tt(ss, q1, zz, ALU.mult)
    mset(q1, -1.0 / 3628800)
    for cf in (1.0 / 40320, -1.0 / 720, 1.0 / 24, -0.5):
        tt(q1, q1, z2, ALU.mult)
        ts(q1, q1, cf, ALU.add)
    tt(vv, q1, z2, ALU.mult)
    ts(vv, vv, -1.0, ALU.mult)
    for _ in range(5):
        ts(q1, vv, -1.0, ALU.mult, 1.0, ALU.add)
        tt(q3, ss, ss, ALU.mult)
        tt(ss, ss, q1, ALU.mult)
        ts(ss, ss, 2.0, ALU.mult)
        ts(vv, q3, 2.0, ALU.mult)
    ts(cc, vv, -1.0, ALU.mult, 1.0, ALU.add)
    tt(are, mag, cc, ALU.mult)
    tt(aim, mag, ss, ALU.mult)
    tt(q1, lr, lr, ALU.mult)
    tt(q3, li, li, ALU.mult)
    tt(q1, q1, q3, ALU.add)
    recip(q1, q1)
    ts(q3, are, -1.0, ALU.add)
    tt(cre, q3, lr, ALU.mult)
    tt(q2, aim, li, ALU.mult)
    tt(cre, cre, q2, ALU.add)
    tt(cre, cre, q1, ALU.mult)
    tt(cim, aim, lr, ALU.mult)
    tt(q2, q3, li, ALU.mult)
    tt(cim, cim, q2, ALU.subtract)
    tt(cim, cim, q1, ALU.mult)
    mset(COS[:], 1.0)
    mset(SIN[:], 0.0)
    cp(COS[:, :, 1], cc)
    cp(SIN[:, :, 1], ss)
    tA = Wb[:, :, :].rearrange("p g m -> p (g m)")[:, 0:1024].rearrange("p (g m) -> p g m", m=32)
    tB = Wpb[:, :, :].rearrange("p g m -> p (g m)")[:, 0:1024].rearrange("p (g m) -> p g m", m=32)
    for k in range(6):
        n = 1 << k
        c0 = bcl(COS[:, :, n], n)
        s0 = bcl(SIN[:, :, n], n)
        src_c = COS[:, :, 1:n + 1]
        src_s = SIN[:, :, 1:n + 1]
        tt(tA[:, :, 0:n], src_c, c0, ALU.mult)
        tt(tB[:, :, 0:n], src_s, s0, ALU.mult)
        tt(COS[:, :, n + 1:2 * n + 1], tA[:, :, 0:n], tB[:, :, 0:n], ALU.subtract)
        tt(tA[:, :, 0:n], src_c, s0, ALU.mult)
        tt(tB[:, :, 0:n], src_s, c0, ALU.mult)
        tt(SIN[:, :, n + 1:2 * n + 1], tA[:, :, 0:n], tB[:, :, 0:n], ALU.add)
    cp(RT[:], bcl(mag, 65))
    mset(RT[:, :, 0], 0.0)
    cp(SINS[:], SIN[:, :, 64])
    ts(SINS[64:128, :], SINS[64:128, :], -1.0, ALU.mult)
    BR = yst[:, 0, :].rearrange("p (g c) -> p g c", c=16)
    BI = yst[:, 1, :].rearrange("p (g c) -> p g c", c=16)
    XS = yst[:, 2, :].rearrange("p (g c) -> p g c", c=16)
    XW = yst[:, 3, :].rearrange("p (g c) -> p g c", c=16)
    for hf in range(2):
        P.dma("sync", BR[hf * 64:hf * 64 + 64], bre_d.rearrange("g p c -> p g c"))
        P.dma("sync", BI[hf * 64:hf * 64 + 64], bim_d.rearrange("g p c -> p g c"))
    crb = bcl(cre, 16)
    cib = bcl(cim, 16)
    tt(XS[:], BR[:], crb, ALU.mult)
    tt(XW[:], BI[:], cib, ALU.mult)
    tt(XS[:], XS[:], XW[:], ALU.subtract)
    tt(XW[:], BI[:], crb, ALU.mult)
    tt(BI[:], BR[:], cib, ALU.mult)
    tt(XW[:], XW[:], BI[:], ALU.add)
    cp(BR[:], XS[:])
    cp(XS[64:128], XW[64:128])
    ts(XW[64:128], BR[64:128], -1.0, ALU.mult)
    for q in range(4):
        for vi, X in enumerate((XS, XW)):
            pb = bank()
            tr(pb[:, 0:128], X[:, q * 8:(q + 1) * 8, :].rearrange("p g c -> p (g c)"), ident[:])
            cp(WS[:, q, vi, :], pb[:, 0:128])
            for g8 in range(8):
                ts(WSg[:, q * 8 + g8, vi, :], WS[:, q, vi, :], par[:, g8:g8 + 1], ALU.mult)
    CL = yst[:, 5, 0:128]
    CWf = yst[:, 4, :]
    for q in range(4):
        P.dma("sync", CL[:, 0:64], cre_d[q * 8:(q + 1) * 8].rearrange("g c p -> (g c) p"))
        P.dma("sync", CL[:, 64:128], cim_d[q * 8:(q + 1) * 8].rearrange("g c p -> (g c) p"))
        pb = bank()
        tr(pb[:, 0:128], CL[:], ident[:])
        cp(CWf[0:64, q * 128:(q + 1) * 128], pb[0:64, 0:128])
        ts(CWf[64:128, q * 128:(q + 1) * 128], pb[64:128, 0:128], -1.0, ALU.mult)
    mset(CWp[:], 0.0)
    c4 = CWf[:, :].rearrange("p (k t c) -> p k t c", t=2, c=16)
    for pr in range(2):
        o4 = CWp[:, pr, :].rearrange("p (k t c) -> p k t c", t=2, c=16)
        cp(o4[:, :, pr, :], c4[:, :, pr, :])
    P.dma("sync", Dcol[:], dd_d.rearrange("(q p) -> p q", p=128), allow_slow_non_contiguous=True)
    mset(car[:], 0.0)
    mset(attnT[:], 0.0)
    mset(memT[:], 0.0)
    mset(ysT[:], 0.0)
    mset(Wb[:], 0.0)
    mset(Wpb[:], 0.0)
    mset(t1[:], 0.0)
    mset(t2[:], 0.0)
    P.dma("sync", kvf[:, 0:64], sre_d)
    P.dma("sync", kvf[:, 64:128], sim_d)
    pb = bank()
    tr(pb[:, 0:128], kvf[:, 0:128], ident[:])
    cp(cars[:, 0, :, :].rearrange("p b g -> p (b g)"), pb[:, 0:128])
    cp(kvf[:, 128:192], kvf[:, 64:128])
    ts(kvf[:, 192:256], kvf[:, 0:64], -1.0, ALU.mult)
    pb = bank()
    tr(pb[:, 0:128], kvf[:, 128:256], ident[:])
    cp(cars[:, 1, :, :].rearrange("p b g -> p (b g)"), pb[:, 0:128])

    dump(sm[:, :, :].rearrange("p a b -> p (a b)"), 512)
    dump(COS[:, :, :].rearrange("p a b -> p (a b)"), 2080)
    dump(SIN[:, :, :].rearrange("p a b -> p (a b)"), 2080)
    dump(RT[:, :, :].rearrange("p a b -> p (a b)"), 2080)
    dump(cars[:, :, :, :].rearrange("p a b c -> p (a b c)"), 256)
    dump(Dcol[:, :], 4)
    dump(yst[:, 6:8, :].rearrange("p a t -> p (a t)"), 1024)
    dump(yst[:, 4, :], 512)
    STAGE = int(os.environ.get("KSTAGE", "99"))

    def finish():
        P.barrier_all("sync", outs)
        P.build()
        print("nsem", P.nsem, "nops", len(P.ops))
        return nc
    if STAGE == 0:
        return finish()

    def load_xT(src_rows, n):
        nb = n // 128
        P.dma("gpsimd", ytok[:, 0:nb, :], src_rows.rearrange("(b p) f -> p b f", p=128), group="xin")
        for b in range(nb):
            for h in range(2):
                pb = bank()
                for j in range(4):
                    tr(pb[:, j * 128:(j + 1) * 128], ytok[:, b, (h * 4 + j) * 128:(h * 4 + j + 1) * 128], ident[:])
                eng = V if (b + h) % 2 == 0 else A
                src = pb[:, :].rearrange("p (j t) -> p j t", t=128)
                if eng == V:
                    cp(xT[:, h * 4:h * 4 + 4, b * 128:(b + 1) * 128], src)
                else:
                    act(xT[:, h * 4:h * 4 + 4, b * 128:(b + 1) * 128], src, AF.Copy)

    def store_xT(dst_rows, n):
        nb = n // 128
        for b in range(nb):
            for h in range(2):
                pb = bank()
                for j in range(4):
                    tr(pb[:, j * 128:(j + 1) * 128], xT[:, h * 4 + j, b * 128:(b + 1) * 128], ident[:])
                if (b + h) % 2 == 0:
                    cp(ytok[:, b, h * 512:(h + 1) * 512], pb[:, :])
                else:
                    act(ytok[:, b, h * 512:(h + 1) * 512], pb[:, :], AF.Copy)
        P.dma("gpsimd", dst_rows.rearrange("(b p) f -> p b f", p=128), ytok[:, 0:nb, :], group="xout")

    def rstd_of(src, n, nch=8):
        pb = bank()
        for c in range(nch):
            s = sqb[c % 2]
            act(s[:, :n], src[:, c, :n], AF.Square)
            mm(pb[:, :n], onesb[:], s[:, :n], start=(c == 0), stop=(c == nch - 1))
        act(rstd[:, :n], pb[:, :n], AF.Sqrt, scale=1.0 / (128 * nch), bias=1e-6)
        recip(rstd[:, :n], rstd[:, :n])

    def prenorm(gn, n, dst=None, src=None):
        dst = hT if dst is None else dst
        src = xT if src is None else src
        rstd_of(src, n)
        for c in range(8):
            stt(dst[:, c, :n], src[:, c, :n], gains[gn][:, c:c + 1], rstd[:, :n], ALU.mult, ALU.mult)

    def postnorm_add(gn, n, half):
        g = hgains[gn] if half else gains[gn]
        rstd_of(yst, n)
        for c in range(8):
            stt(tmpf[:, :n], yst[:, c, :n], g[:, c:c + 1], rstd[:, :n], ALU.mult, ALU.mult)
            tt(xT[:, c, :n], xT[:, c, :n], tmpf[:, :n], ALU.add, eng=(G if c % 2 else V))

    def ffn_gen(pre, n):
        for hb in range(11):
            wv = wload(pre + "_in%d" % hb)
            for h2 in range(2):
                pg = bank()
                pu = bank()
                for kc in range(8):
                    mm(pg[:, :n], wv[:, kc, h2 * 128:h2 * 128 + 128], hT[:, kc, :n], start=(kc == 0), stop=(kc == 7))
                for kc in range(8):
                    mm(pu[:, :n], wv[:, kc, 256 + h2 * 128:256 + h2 * 128 + 128], hT[:, kc, :n], start=(kc == 0), stop=(kc == 7))
                tb = tmpb[(hb * 2 + h2) % 2]
                act(tb[:, :n], pg[:, :n], AF.Silu)
                tt(hid[:, hb * 2 + h2, :n], tb[:, :n], pu[:, :n], ALU.mult)
            yield
        for m in range(8):
            wv = wload(pre + "_out%d" % m)
            po = bank()
            for kc in range(22):
                mm(po[:, :n], wv[:, kc, :], hid[:, kc, :n], start=(kc == 0), stop=(kc == 21))
            act(yst[:, m, :n], po[:, :n], AF.Copy)
            yield

    def ffn(pre, n):
        for _ in ffn_gen(pre, n):
            pass

    def u_proj(n, dst=None):
        dst = uT if dst is None else dst
        wv = wload("in_u")
        for m in range(4):
            pb = bank()
            for kc in range(8):
                mm(pb[:, :n], wv[:, kc, m * 128:(m + 1) * 128], hT[:, kc, :n], start=(kc == 0), stop=(kc == 7))
            act(dst[:, m, :n], pb[:, :n], AF.Copy)

    def ttk(out, a, b, op, eng, r, w):
        P.add(eng, lambda e: e.tensor_tensor(out, a, b, op), r=r, w=w)

    def cpk(out, a, eng, r, w):
        P.add(eng, lambda e: e.tensor_copy(out, a), r=r, w=w)

    pending = []

    def flush_pending():
        while pending:
            pending.pop(0)()

    def ssm_block(c0, n, cs, csw, full, usrc=None, halves=None):
        usrc = uT if usrc is None else usrc
        WbK = [("Wb", 0), ("Wb", 1), ("Wb", "c0")]
        WpK = [("Wp", 0), ("Wp", 1), ("Wp", "c0")]
        t1K = [("t1", 0), ("t1", 1)]
        t2K = [("t2", 0), ("t2", 1)]
        for hf in (range(2) if halves is None else halves):
            G0 = hf * 16
            pSv = [b[:, :].rearrange("p (g t) -> p g t", t=64) for b in pSb]
            pWv = [b[:, :].rearrange("p (g t) -> p g t", t=64) for b in pWb]
            for gi in range(16):
                g = G0 + gi
                q = g // 8
                rhs = usrc[:, q, c0:c0 + n]
                mm(pSv[gi // 8][:, gi % 8, :n], WSg[:, g, 0, :], rhs)
                mm(pWv[gi // 8][:, gi % 8, :n], WSg[:, g, 1, :], rhs)
            if (not full) and n == 64:
                cpk(Wb[:, :, 0], cs[:, G0:G0 + 16], G, [cs], [("Wb", "c0")])
                for hh in range(2):
                    gs = slice(hh * 8, hh * 8 + 8)
                    cs_ = COS[:, G0 + hh * 8:G0 + hh * 8 + 8, 1:n + 1]
                    sn_ = SIN[:, G0 + hh * 8:G0 + hh * 8 + 8, 1:n + 1]
                    ttk(Wb[:, gs, 1:n + 1], pSv[hh][:, :, :n], cs_, ALU.mult, V, [pSb[hh], COS], [("Wb", hh)])
                    ttk(t1[:, gs, 1:n + 1], pWv[hh][:, :, :n], sn_, ALU.mult, V, [pWb[hh], SIN], [("t1", hh)])
                    ttk(Wb[:, gs, 1:n + 1], Wb[:, gs, 1:n + 1], t1[:, gs, 1:n + 1], ALU.add, G, [("Wb", hh), ("t1", hh)], [("Wb", hh)])
                rflat = RT[:, G0:G0 + 16, :].rearrange("p g m -> p (g m)")
                sf = Wb[:, :, :].rearrange("p g m -> p (g m)")
                df = t1[:, :, :].rearrange("p g m -> p (g m)")
                P.add(V, lambda e, sf=sf, df=df, rflat=rflat: e.tensor_tensor_scan(out=df, data0=rflat, data1=sf, initial=0.0, op0=ALU.mult, op1=ALU.add),
                      r=[RT] + WbK, w=t1K)
                wn, wsw = wnT[hf], wswT[hf]
                flush_pending()
                cpk(wn[:, :], t1[:, :, n], G, t1K, [wn])
                P.dma("gpsimd", wsw[0:64, :], wn[64:128, :])
                P.dma("gpsimd", wsw[64:128, :], wn[0:64, :])

                def fin(wn=wn, wsw=wsw, G0=G0, cs=cs, n=n):
                    a1 = sm[:, 13, 0:16]
                    a2 = sm[:, 14, 0:16]
                    ttk(a1, wn[:, :], COS[:, G0:G0 + 16, n], ALU.mult, G, [wn, COS], [("a1",)])
                    ttk(a2, wsw[:, :], SINS[:, G0:G0 + 16], ALU.mult, G, [wsw, SINS], [("a2",)])
                    ttk(cs[:, G0:G0 + 16], a1, a2, ALU.subtract, G, [("a1",), ("a2",)], [cs])
                pending.append(fin)
                continue
            cpk(Wb[:, :, 0], cs[:, G0:G0 + 16], G, [cs], [("Wb", "c0")])
            cpk(Wpb[:, :, 0], csw[:, G0:G0 + 16], G, [csw], [("Wp", "c0")])
            for hh in range(2):
                gs = slice(hh * 8, hh * 8 + 8)
                cs_ = COS[:, G0 + hh * 8:G0 + hh * 8 + 8, 1:n + 1]
                sn_ = SIN[:, G0 + hh * 8:G0 + hh * 8 + 8, 1:n + 1]
                ttk(Wb[:, gs, 1:n + 1], pSv[hh][:, :, :n], cs_, ALU.mult, V, [pSb[hh], COS], [("Wb", hh)])
                ttk(t1[:, gs, 1:n + 1], pWv[hh][:, :, :n], sn_, ALU.mult, V, [pWb[hh], SIN], [("t1", hh)])
                ttk(Wb[:, gs, 1:n + 1], Wb[:, gs, 1:n + 1], t1[:, gs, 1:n + 1], ALU.add, G, [("Wb", hh), ("t1", hh)], [("Wb", hh)])
            for hh in range(2):
                gs = slice(hh * 8, hh * 8 + 8)
                cs_ = COS[:, G0 + hh * 8:G0 + hh * 8 + 8, 1:n + 1]
                sn_ = SIN[:, G0 + hh * 8:G0 + hh * 8 + 8, 1:n + 1]
                ttk(Wpb[:, gs, 1:n + 1], pWv[hh][:, :, :n], cs_, ALU.mult, V, [pWb[hh], COS], [("Wp", hh)])
                ttk(t2[:, gs, 1:n + 1], pSv[hh][:, :, :n], sn_, ALU.mult, V, [pSb[hh], SIN], [("t2", hh)])
                ttk(Wpb[:, gs, 1:n + 1], Wpb[:, gs, 1:n + 1], t2[:, gs, 1:n + 1], ALU.subtract, G, [("Wp", hh), ("t2", hh)], [("Wp", hh)])
            rflat = RT[:, G0:G0 + 16, :].rearrange("p g m -> p (g m)")
            for (src, dst, sk, dk) in ((Wb, t1, WbK, t1K), (Wpb, t2, WpK, t2K)):
                sf = src[:, :, :].rearrange("p g m -> p (g m)")
                df = dst[:, :, :].rearrange("p g m -> p (g m)")
                P.add(V, lambda e, sf=sf, df=df, rflat=rflat: e.tensor_tensor_scan(out=df, data0=rflat, data1=sf, initial=0.0, op0=ALU.mult, op1=ALU.add),
                      r=[RT] + sk, w=dk)
            cn = COS[:, G0:G0 + 16, n]
            snn = SIN[:, G0:G0 + 16, n]
            a1 = sm[:, 13, 0:16]
            a2 = sm[:, 14, 0:16]
            ttk(a1, t1[:, :, n], cn, ALU.mult, G, t1K + [COS], [("a1",)])
            ttk(a2, t2[:, :, n], snn, ALU.mult, G, t2K + [SIN], [("a2",)])
            ttk(cs[:, G0:G0 + 16], a1, a2, ALU.subtract, G, [("a1",), ("a2",)], [cs])
            ttk(a1, t2[:, :, n], cn, ALU.mult, G, t2K + [COS], [("a1",)])
            ttk(a2, t1[:, :, n], snn, ALU.mult, G, t1K + [SIN], [("a2",)])
            ttk(csw[:, G0:G0 + 16], a1, a2, ALU.add, G, [("a1",), ("a2",)], [csw])
            if not full:
                continue
            ca = COS[:, G0:G0 + 16, 0:n + 1]
            sa = SIN[:, G0:G0 + 16, 0:n + 1]
            ttk(Wb[:, :, 0:n + 1], t1[:, :, 0:n + 1], ca, ALU.mult, V, t1K + [COS], WbK)
            ttk(Wpb[:, :, 0:n + 1], t2[:, :, 0:n + 1], sa, ALU.mult, G, t2K + [SIN], WpK)
            ttk(sbf[:, :, 0:n + 1], Wb[:, :, 0:n + 1], Wpb[:, :, 0:n + 1], ALU.subtract, V, WbK + WpK, [sbf])
            py = bank()
            pyv = py[:, 0:128].rearrange("p (q t) -> p q t", t=64)
            for gi in range(16):
                g = G0 + gi
                q, w, pr = g // 8, (g % 8) // 2, g % 2
                mm(pyv[32 * w:32 * w + 32, q - 2 * hf, :n], CWp[:, pr, q * 128 + 32 * w:q * 128 + 32 * w + 32], sbf[:, gi, 1:n + 1],
                   start=(pr == 0), stop=(pr == 1), tp=(0, 32 * w))
            for ql in range(2):
                q = 2 * hf + ql
                stt(ysT[:, q, c0:c0 + n], usrc[:, q, c0:c0 + n], Dcol[:, q:q + 1], pyv[:, ql, :n], ALU.mult, ALU.add)

    def cross_attn(c0, n, mk, mv):
        for h in range(4):
            E = Eb[h % 2]
            for mb in range(2):
                pb = bank()
                mm(pb[:, :n], mk[:, h, mb * 128:(mb + 1) * 128], qmT[:, h, c0:c0 + n])
                act(E[:, mb, :n], pb[:, :n], AF.Exp, scale=128 ** -0.5)
            po = bank()
            pd = bank()
            for mb in range(2):
                mm(po[:, :n], mv[:, mb, h * 128:(h + 1) * 128], E[:, mb, :n], start=(mb == 0), stop=(mb == 1))
            for mb in range(2):
                mm(pd[:, :n], onesb[:], E[:, mb, :n], start=(mb == 0), stop=(mb == 1))
            recip(tmpf[:, :n], pd[:, :n])
            tt(memT[:, h, c0:c0 + n], po[:, :n], tmpf[:, :n], ALU.mult)

    def swa_unit(ps, kT_ap, q_ap, nq, bias_ap, v_ap, ones_ap, po_ap, pd_ap, E, poff):
        o = ps[:, 0:nq]
        mm(o, kT_ap, q_ap, start=True, stop=False)
        mm(o, identb[:, :], bias_ap, start=False, stop=True)
        e = E[:, 0, 0:nq]
        act(e, o, AF.Exp, scale=0.125)
        mm(po_ap, v_ap, e, start=False, stop=False, tp=(0, poff))
        mm(pd_ap, ones_ap, e, start=False, stop=False, tp=(0, poff))

    def swa_finish(hp, po, pd, c0, n):
        ts(tmpf[:, :n], pd[:, :n], esink[:, hp:hp + 1], ALU.add)
        recip(tmpf[:, :n], tmpf[:, :n])
        tt(attnT[:, hp, c0:c0 + n], po[:, :n], tmpf[:, :n], ALU.mult)

    def swa_prompt(n, first):
        nqc = n // 64
        for hp in range(4):
            po = banks[0]
            pd = banks[1]
            mm(po[:, :n], zer[:, 0:128], zer[:, :n])
            mm(pd[:, :n], zer[:, 0:128], zer[:, :n])
            kvh = hp // 2
            cnt = 0
            for kb in range(nqc // 2 + 1):
                qlo, qhi = max(2 * kb - 2, 0), min(2 * kb + 1, nqc - 1)
                nq = (qhi - qlo + 1) * 64
                r0 = qlo - (2 * kb - 2)
                for hh in range(2):
                    h = 2 * hp + hh
                    poff = hh * 64
                    ps = banks[2 + cnt % 2]
                    ones_ap = onesv[:, :] if (first and kb == 0) else onesb[:, 0:64]
                    swa_unit(ps, kTz[kvh][hh][:, kb * 128:(kb + 1) * 128],
                             qT[:, hp, qlo * 64:qlo * 64 + nq], nq,
                             biasP[:, h, r0 * 64:r0 * 64 + nq],
                             vtok[:, kb, kvh * 64:(kvh + 1) * 64], ones_ap,
                             po[poff:poff + 64, qlo * 64:qlo * 64 + nq], pd[poff:poff + 64, qlo * 64:qlo * 64 + nq], Eb[cnt % 2], poff)
                    cnt += 1
            swa_finish(hp, po, pd, 0, n)

    def kv_proj(n, c0k, vb0, want_out=None):
        wv = wload("in_kv")
        for kvh in range(2):
            pb = bank()
            for dup in range(2):
                for kc in range(8):
                    mm(pb[dup * 64:dup * 64 + 64, :n], wv[:, kc, kvh * 64:kvh * 64 + 64], hT[:, kc, :n], start=(kc == 0), stop=(kc == 7), tp=(0, dup * 64))
            act(kTz[kvh][0][0:64, c0k:c0k + n], pb[0:64, :n], AF.Copy)
            cp(kTz[kvh][1][64:128, c0k:c0k + n], pb[64:128, :n])
        for tb in range(n // 128):
            pb = bank()
            for kc in range(8):
                mm(pb[:, 0:256], hT[:, kc, tb * 128:(tb + 1) * 128], wv[:, kc, :], start=(kc == 0), stop=(kc == 7))
            cp(vtok[:, vb0 + tb, :], pb[:, 128:256])
            if want_out is not None and tb in want_out:
                ko, vo = want_out[tb]
                cp(kvf[:, :], pb[:, 0:256])
                P.dma("gpsimd", ko, kvf[:, 0:128], group="kvout")
                P.dma("gpsimd", vo, kvf[:, 128:256], group="kvout")

    def q_proj(n):
        for (bn, dst) in (("in_q", qT), ("in_qm", qmT)):
            wv = wload(bn)
            for m in range(4):
                pb = bank()
                for kc in range(8):
                    mm(pb[:, :n], wv[:, kc, m * 128:(m + 1) * 128], hT[:, kc, :n], start=(kc == 0), stop=(kc == 7))
                if m % 2 == 0:
                    act(dst[:, m, :n], pb[:, :n], AF.Copy)
                else:
                    cp(dst[:, m, :n], pb[:, :n])

    def merge_out(n):
        macc = yst
        wA = None
        for bi, br in enumerate((1, 0, 2)):
            if br == 0:
                wbr = wload("abr")
                src = attnT
            elif br == 2:
                wbr = wload("mbr")
                src = memT
            for jh in range(2):
                gw = wload("in_g%d" % (br * 2 + jh))
                if br == 1:
                    ga = wload("glu%d" % jh)
                    gb = wload("glu%d" % (2 + jh))
                for jj in range(4):
                    j = jh * 4 + jj
                    pg = bank()
                    for kc in range(8):
                        mm(pg[:, :n], gw[:, kc, jj * 128:(jj + 1) * 128], hT[:, kc, :n], start=(kc == 0), stop=(kc == 7))
                    sg = tmpb[0]
                    act(sg[:, :n], pg[:, :n], AF.Sigmoid)
                    if br == 1:
                        pa = bank()
                        pbk = bank()
                        for kc in range(4):
                            mm(pa[:, :n], ga[:, kc, jj * 128:(jj + 1) * 128], ysT[:, kc, :n], start=(kc == 0), stop=(kc == 3))
                        for kc in range(4):
                            mm(pbk[:, :n], gb[:, kc, jj * 128:(jj + 1) * 128], ysT[:, kc, :n], start=(kc == 0), stop=(kc == 3))
                        sb2 = tmpb[1]
                        act(sb2[:, :n], pbk[:, :n], AF.Sigmoid)
                        tt(tmpf[:, :n], pa[:, :n], sb2[:, :n], ALU.mult)
                        tt(macc[:, j, :n], tmpf[:, :n], sg[:, :n], ALU.mult)
                    else:
                        pa = bank()
                        for kc in range(4):
                            mm(pa[:, :n], wbr[:, kc, j * 128:(j + 1) * 128], src[:, kc, :n], start=(kc == 0), stop=(kc == 3))
                        tt(tmpf[:, :n], pa[:, :n], sg[:, :n], ALU.mult)
                        tt(macc[:, j, :n], macc[:, j, :n], tmpf[:, :n], ALU.add, eng=G)
        for c in range(8):
            act(mergedT[:, c, :n], macc[:, c, :n], AF.Copy)
        for i in range(2):
            wv = wload("out%d" % i)
            for mmi in range(4):
                m = i * 4 + mmi
                po = bank()
                for kc in range(8):
                    mm(po[:, :n], wv[:, kc, mmi * 128:(mmi + 1) * 128], mergedT[:, kc, :n], start=(kc == 0), stop=(kc == 7))
                act(yst[:, m, :n], po[:, :n], AF.Copy)

    memn = mergedT[:, :, 0:256]
    memf = xT
    P.dma("gpsimd", ytok[:, 0:2, :], mem_d.rearrange("(b p) f -> p b f", p=128), group="xin")
    for b in range(2):
        for h in range(2):
            pb = bank()
            for j in range(4):
                tr(pb[:, j * 128:(j + 1) * 128], ytok[:, b, (h * 4 + j) * 128:(h * 4 + j + 1) * 128], ident[:])
            cp(memf[:, h * 4:h * 4 + 4, b * 128:(b + 1) * 128], pb[:, :].rearrange("p (j t) -> p j t", t=128))
    prenorm("mem_norm_g", 256, dst=mergedT, src=memf)
    wk0 = wload("mkv0")
    wk1 = wload("mkv1")
    for h in range(4):
        pb = bank()
        for kc in range(8):
            mm(pb[:, 0:256], wk0[:, kc, h * 128:(h + 1) * 128], memn[:, kc, :], start=(kc == 0), stop=(kc == 7))
        cp(mkT[:, h, :], pb[:, 0:256])
    for mb in range(2):
        for wi, (wv, oo) in enumerate(((wk0, mkp_o), (wk1, mvp_o))):
            pb = bank()
            for kc in range(8):
                mm(pb[:, :], memn[:, kc, mb * 128:(mb + 1) * 128], wv[:, kc, :], start=(kc == 0), stop=(kc == 7))
            cp(cstage[:, wi, :], pb[:, :])
            if wi == 1:
                act(mvt[:, mb, :], pb[:, :], AF.Copy)
            P.dma("gpsimd", oo[mb * 128:(mb + 1) * 128, :], cstage[:, wi, :], group="memout")

    WbK_ = [("Wb", 0), ("Wb", 1), ("Wb", "c0")]
    t2K_ = [("t2", 0), ("t2", 1)]
    lstate = {"A": None, "G": None}

    def light_A(h, usrc):
        blk, hf = h
        c0, n, G0 = blk * 64, 64, hf * 16
        cs = car[:, 0, :]
        pSv = [b[:, :].rearrange("p (g t) -> p g t", t=64) for b in pSb]
        pWv = [b[:, :].rearrange("p (g t) -> p g t", t=64) for b in pWb]
        for gi in range(16):
            g = G0 + gi
            rhs = usrc[:, g // 8, c0:c0 + n]
            mm(pSv[gi // 8][:, gi % 8, :n], WSg[:, g, 0, :], rhs)
            mm(pWv[gi // 8][:, gi % 8, :n], WSg[:, g, 1, :], rhs)
        cpk(Wb[:, :, 0], cs[:, G0:G0 + 16], G, [cs], [("Wb", "c0")])
        for hh in range(2):
            gs = slice(hh * 8, hh * 8 + 8)
            cs_ = COS[:, G0 + hh * 8:G0 + hh * 8 + 8, 1:n + 1]
            sn_ = SIN[:, G0 + hh * 8:G0 + hh * 8 + 8, 1:n + 1]
            ttk(Wb[:, gs, 1:n + 1], pSv[hh][:, :, :n], cs_, ALU.mult, V, [pSb[hh], COS], [("Wb", hh)])
            ttk(t1[:, gs, 1:n + 1], pWv[hh][:, :, :n], sn_, ALU.mult, V, [pWb[hh], SIN], [("t1", hh)])
            ttk(Wb[:, gs, 1:n + 1], Wb[:, gs, 1:n + 1], t1[:, gs, 1:n + 1], ALU.add, G, [("Wb", hh), ("t1", hh)], [("Wb", hh)])

    def light_scan(h):
        blk, hf = h
        G0 = hf * 16
        rflat = RT[:, G0:G0 + 16, :].rearrange("p g m -> p (g m)")
        sf = Wb[:, :, :].rearrange("p g m -> p (g m)")
        df = t2[:, :, :].rearrange("p g m -> p (g m)")
        P.add(V, lambda e, sf=sf, df=df, rflat=rflat: e.tensor_tensor_scan(out=df, data0=rflat, data1=sf, initial=0.0, op0=ALU.mult, op1=ALU.add),
              r=[RT] + WbK_, w=t2K_)

    def light_gtail(h):
        blk, hf = h
        wn, wsw = wnT[hf], wswT[hf]
        cpk(wn[:, :], t2[:, :, 64], G, t2K_, [wn])
        P.dma("gpsimd", wsw[0:64, :], wn[64:128, :])
        P.dma("gpsimd", wsw[64:128, :], wn[0:64, :])

    def light_fin(h):
        blk, hf = h
        G0 = hf * 16
        wn, wsw = wnT[hf], wswT[hf]
        cs = car[:, 0, :]
        a1 = sm[:, 13, 0:16]
        a2 = sm[:, 14, 0:16]
        ttk(a1, wn[:, :], COS[:, G0:G0 + 16, 64], ALU.mult, G, [wn, COS], [("a1",)])
        ttk(a2, wsw[:, :], SINS[:, G0:G0 + 16], ALU.mult, G, [wsw, SINS], [("a2",)])
        ttk(cs[:, G0:G0 + 16], a1, a2, ALU.subtract, G, [("a1",), ("a2",)], [cs])

    def light_slot(h, usrc=None):
        pa, pg = lstate["A"], lstate["G"]
        if pa is not None:
            light_scan(pa)
        if pg is not None:
            light_fin(pg)
        if h is not None:
            light_A(h, usrc)
        if pa is not None:
            light_gtail(pa)
        lstate["G"] = pa
        lstate["A"] = h

    if STAGE == 1:
        return finish()
    uTl = mergedT[:, 0:4, :]
    for it in range(nlight):
        load_xT(xl[it * TT:(it + 1) * TT, :], TT)
        prenorm("ff1_pre_g", TT)
        gen = ffn_gen("ff1", TT)
        k = 0
        for _ in gen:
            if it > 0 and k < 16:
                light_slot((k // 2, k % 2), uTl)
            k += 1
        postnorm_add("ff1_post_g", TT, True)
        prenorm("mix_pre_g", TT)
        u_proj(TT, dst=uTl)
    if nlight > 0:
        for k in range(16):
            light_slot((k // 2, k % 2), uTl)
        light_slot(None)
        light_slot(None)
        P.dma("sync", car[0:64, 1, :], car[64:128, 0, :])
        P.dma("sync", car[64:128, 1, :], car[0:64, 0, :])
        ts(car[64:128, 1, :], car[64:128, 1, :], -1.0, ALU.mult)
    if STAGE == 2:
        return finish()
    def full_tile(ti):
        extra = (ti == 0)
        n = 256 if extra else TT
        r0 = 0 if extra else 256 + (ti - 1) * TT
        last = (ti == nseg)
        load_xT(xf[r0:r0 + n, :], n)
        prenorm("ff1_pre_g", n)
        ffn("ff1", n)
        postnorm_add("ff1_post_g", n, True)
        prenorm("mix_pre_g", n)
        if extra:
            kv_proj(256, 0, 0, want_out={1: (ks_o, vs_o)})
            ts(vtok[:, 0, :], vtok[:, 0, :], valid[:, 0:1], ALU.mult)
        else:
            wo = {3: (kp_o, vp_o)} if last else None
            kv_proj(n, 128, 1, want_out=wo)
        q_proj(n)
        u_proj(n)
        if extra:
            for sq in range(4):
                c0 = 128 + 32 * sq
                ssm_block(c0, 32, cars[:, 0, sq, :], cars[:, 1, sq, :], True)
                KEX = int(os.environ.get("KEXTRA", "99"))
                if KEX == 1:
                    continue
                P.dma("gpsimd", cstage[:, :, :], cmk_d[sq].rearrange("(b p) f -> p b f", p=128), group="cst")
                mks = P_mks
                for mb in range(2):
                    pb = bank()
                    for h in range(4):
                        tr(pb[:, h * 128:(h + 1) * 128], cstage[:, mb, h * 128:(h + 1) * 128], ident[:])
                    cp(mks[:, :, mb * 128:(mb + 1) * 128], pb[:, :].rearrange("p (h t) -> p h t", t=128))
                P.dma("gpsimd", P_mvs[:, :, :], cmv_d[sq].rearrange("(b p) f -> p b f", p=128), group="cmv")
                cross_attn(c0, 32, mks, P_mvs)
                if KEX == 2:
                    continue
                P.dma("gpsimd", cstage[:, 0, 0:128], csk_d[sq], group="cst")
                P.dma("gpsimd", csb[:, 1, :], csv_d[sq], group="csb")
                kpad = P_kpad_holder[0]
                for kvh in range(2):
                    cp(kpad[:, 0, 0:64], cstage[:, 0, kvh * 64:(kvh + 1) * 64])
                    cp(kpad[:, 1, 64:128], cstage[:, 0, kvh * 64:(kvh + 1) * 64])
                    for hh in range(2):
                        pbt = bank()
                        tr(pbt[:, 0:128], kpad[:, hh, :], ident[:])
                        cp(P_ckT[kvh][hh][:, :], pbt[:, 0:128])
                for hp in range(4):
                    po = banks[0]
                    pd = banks[1]
                    mm(po[:, :32], zer[:, 0:128], zer[:, :32])
                    mm(pd[:, :32], zer[:, 0:128], zer[:, :32])
                    kvh = hp // 2
                    cnt = 0
                    for hh in range(2):
                        h = 2 * hp + hh
                        poff = hh * 64
                        qa = qT[:, hp, c0:c0 + 32]
                        for j in range(2):
                            ps = banks[2 + cnt % 2]
                            if j == 0:
                                kT_ap = P_ckT[kvh][hh][:, :]
                                v_ap = csb[:, 1, kvh * 64:(kvh + 1) * 64]
                                b_ap = biasSc[:, h, :]
                            else:
                                kT_ap = kTz[kvh][hh][:, 128:256]
                                v_ap = vtok[:, 1, kvh * 64:(kvh + 1) * 64]
                                b_ap = biasSn[:, h, 32 * sq:32 * sq + 32]
                            swa_unit(ps, kT_ap, qa, 32, b_ap, v_ap, onesb[:, 0:64],
                                     po[poff:poff + 64, 0:32], pd[poff:poff + 64, 0:32], Eb[cnt % 2], poff)
                            if DBG and sq == 0 and hp == 0:
                                cp(tmpf[:, 0:32], ps[:, 0:32])
                                dump(tmpf[:, 0:32], 32)
                            cnt += 1
                    if DBG and sq == 0 and hp == 0:
                        cp(tmpf[:, 0:32], po[:, 0:32])
                        dump(tmpf[:, 0:32], 32)
                        cp(tmpf[:, 0:32], pd[:, 0:32])
                        dump(tmpf[:, 0:32], 32)
                    swa_finish(hp, po, pd, c0, 32)
            for vi in range(1):
                pb = bank()
                tr(pb[:, 0:128], cars[:, 0, :, :].rearrange("p b g -> p (b g)"), ident[:])
                cp(kvf[:, 0:128], pb[:, 0:128])
                P.dma("gpsimd", sss_o, kvf[:, 0:128], group="kvout")
        else:
            for blk in range(8):
                ssm_block(blk * 64, 64, car[:, 0, :], car[:, 1, :], True)
            cross_attn(0, n, mkT, mvt)
            swa_prompt(n, first=(ti == 1))
            for kvh in range(2):
                for hh in range(2):
                    cp(kTz[kvh][hh][hh * 64:hh * 64 + 64, 0:128], kTz[kvh][hh][hh * 64:hh * 64 + 64, n:n + 128], eng=G)
            cp(vtok[:, 0, :], vtok[:, 4, :], eng=G)
            if last:
                pb = bank()
                tr(pb[0:32, 0:128], car[:, 0, :], ident[:])
                cp(kvf[0:32, 0:128], pb[0:32, 0:128])
                P.dma("gpsimd", ssp_o, kvf[0:32, 0:128], group="kvout")
        if extra:
            for buf in (uT, ysT, attnT, memT):
                for q in range(4):
                    cp(tmpf[:, 0:128], buf[:, q, 128:256])
                    dump(tmpf[:, 0:128], 128)
        merge_out(n)
        if extra:
            for c in range(8):
                dump(yst[:, c, 128:256], 128)
        postnorm_add("mix_post_g", n, False)
        prenorm("ff2_pre_g", n)
        ffn("ff2", n)
        postnorm_add("ff2_post_g", n, True)
        store_xT(yf[r0:r0 + n, :], n)

    P_mks = arena[:, 20 * TT:22 * TT].rearrange("p (h m) -> p h m", m=256)
    P_mvs = mergedT[:, 0:2, :]
    P_ckT = [[P.sb("ckT%d%d" % (i, j), [128, 128], BF16) for j in range(2)] for i in range(2)]
    for ti in range(nseg + 1):
        if STAGE == 3 + ti:
            return finish()
        full_tile(ti)
    return finish()


def _t5_bucket(rel):
    half = 16
    max_exact = 8
    ret = (rel > 0).astype(np.int32) * half
    n = np.abs(rel)
    large = max_exact + (np.log(np.maximum(n, 1) / max_exact) / math.log(128 / max_exact) * (half - max_exact)).astype(np.int32)
    large = np.minimum(large, half - 1)
    return ret + np.where(n < max_exact, n, large)


_NC_CACHE = {}


def kernel(**inp):
    f = lambda a: np.ascontiguousarray(np.asarray(a, dtype=np.float32))
    xp = f(inp["x_prompt"])
    xs = f(inp["x_sample"])
    table = f(inp["rel_bias_table"])
    NEG = -1e30
    i = np.arange(64)[:, None]
    j = np.arange(192)[None, :]
    bp = table[_t5_bucket((j - 128) - i)]
    biasP = np.full((8, 128, 256), NEG, np.float32)
    for cb in range(2):
        for r in range(4):
            jb = cb + 2 - r
            if 0 <= jb <= 2:
                biasP[:, cb * 64:(cb + 1) * 64, r * 64:(r + 1) * 64] = bp[:, jb * 64:(jb + 1) * 64, :].transpose(2, 1, 0)
    i = np.arange(32)[:, None]
    j = np.arange(160)[None, :]
    bs = table[_t5_bucket((j - 128) - i)]
    bsT = bs.transpose(2, 1, 0)
    biasSc = np.ascontiguousarray(bsT[:, 0:128, :])
    biasSn = np.full((8, 128, 128), NEG, np.float32)
    for sq in range(4):
        biasSn[:, 32 * sq:32 * sq + 32, 32 * sq:32 * sq + 32] = bsT[:, 128:160, :]
    par = np.zeros((128, 8), np.float32)
    par[np.arange(128), np.arange(128) // 16] = 1.0
    shared = {"biasP": biasP, "biasSc": biasSc, "biasSn": biasSn, "ident": np.eye(128, dtype=np.float32), "par": par,
              "attn_sink": f(inp["attn_sink"]).reshape(8), "ssm_lambda_re": f(inp["ssm_lambda_re"])[0],
              "ssm_lambda_im": f(inp["ssm_lambda_im"])[0], "ssm_log_dt": f(inp["ssm_log_dt"]).reshape(32),
              "ssm_b_re": f(inp["ssm_b_re"])[0], "ssm_b_im": f(inp["ssm_b_im"])[0], "ssm_c_re": f(inp["ssm_c_re"])[0],
              "ssm_c_im": f(inp["ssm_c_im"])[0], "ssm_d": f(inp["ssm_d"]).reshape(512)}
    for n in ("ff1_pre_g", "ff1_post_g", "mix_pre_g", "mix_post_g", "mem_norm_g", "ff2_pre_g", "ff2_post_g"):
        shared[n] = f(inp[n]).reshape(D)
    for n in WSHAPES:
        shared[n] = f(inp[n])[0]
    in_maps = []
    SEQL = xp.shape[1]
    SEG = SEQL // 4
    nseg = SEG // TT
    nlight = 3 * nseg
    for c in range(NCORES):
        b, seg = c // 4, c % 4
        st = seg * SEG
        xl = np.zeros((nlight * TT, D), np.float32)
        if st > 0:
            xl[nlight * TT - st:] = xp[b, :st]
        xf = np.zeros((256 + SEG, D), np.float32)
        if st > 0:
            xf[0:128] = xp[b, st - 128:st]
        xf[128:256] = xs[4 * c:4 * c + 4].reshape(128, D)
        xf[256:] = xp[b, st:st + SEG]
        m = dict(shared)
        m.update({"xl": xl, "xf": xf, "valid": np.full((128, 1), 1.0 if seg > 0 else 0.0, np.float32),
                  "csk": f(inp["cache_swa_k"])[0, 4 * c:4 * c + 4].reshape(4, 128, 128),
                  "csv": f(inp["cache_swa_v"])[0, 4 * c:4 * c + 4].reshape(4, 128, 128),
                  "cmk": f(inp["cache_mem_k"])[0, 4 * c:4 * c + 4].reshape(4, 256, 512),
                  "cmv": f(inp["cache_mem_v"])[0, 4 * c:4 * c + 4].reshape(4, 256, 512),
                  "sre": f(inp["state_ssm_re"])[0, 4 * c:4 * c + 4].reshape(128, 64),
                  "sim": f(inp["state_ssm_im"])[0, 4 * c:4 * c + 4].reshape(128, 64),
                  "mem": f(inp["mem_prompt"])[b]})
        in_maps.append(m)
    key = (nlight, nseg)
    if key not in _NC_CACHE:
        _NC_CACHE[key] = build_nc(nlight, nseg)
    res = run_bass_kernel_spmd(_NC_CACHE[key], in_maps, core_ids=list(range(NCORES)))
    R = res.results
    _NC_CACHE["last"] = R
    yp = np.zeros((2, SEQL, D), np.float32)
    ys = np.zeros((32, 32, D), np.float32)
    for c in range(NCORES):
        b, seg = c // 4, c % 4
        yp[b, seg * SEG:(seg + 1) * SEG] = R[c]["yf"][256:]
        ys[4 * c:4 * c + 4] = R[c]["yf"][128:256].reshape(4, 32, D)
    kp = np.stack([R[4 * b + 3]["kp"].reshape(128, 2, 64) for b in range(2)])[None]
    vp = np.stack([R[4 * b + 3]["vp"].reshape(128, 2, 64) for b in range(2)])[None]
    mkp = np.stack([R[4 * b]["mkp"].reshape(256, 4, 128) for b in range(2)])[None]
    mvp = np.stack([R[4 * b]["mvp"].reshape(256, 4, 128) for b in range(2)])[None]
    srp = np.stack([R[4 * b + 3]["ssp"][:, 0:64] for b in range(2)])[None]
    sip = np.stack([R[4 * b + 3]["ssp"][:, 64:128] for b in range(2)])[None]
    ksn = np.concatenate([R[c]["ks"].reshape(4, 32, 2, 64) for c in range(NCORES)])[None]
    vsn = np.concatenate([R[c]["vs"].reshape(4, 32, 2, 64) for c in range(NCORES)])[None]
    srs = np.concatenate([R[c]["sss"][:, 0:64].reshape(4, 32, 64) for c in range(NCORES)])[None]
    sis = np.concatenate([R[c]["sss"][:, 64:128].reshape(4, 32, 64) for c in range(NCORES)])[None]
    return (yp, ys, np.ascontiguousarray(kp), np.ascontiguousarray(vp), np.ascontiguousarray(mkp), np.ascontiguousarray(mvp),
            np.ascontiguousarray(srp), np.ascontiguousarray(sip), ksn, vsn, srs, sis)
```

```python
import contextlib
import math
import numpy as np
import concourse.bass as bass
import concourse.mybir as mybir
from concourse.bass_utils import run_bass_kernel_spmd

F32 = mybir.dt.float32
BF16 = mybir.dt.bfloat16
AF = mybir.ActivationFunctionType
ALU = mybir.AluOpType
NCORES = 8
D = 1024
DFF = 2816
SEG = 4096
NLIGHT = 24
TT = 512

STREAMS = ("tensor", "scalar", "vector", "gpsimd", "sync")
DEEP = ("scalar", "vector", "gpsimd")


def _key(x):
    if isinstance(x, (str, tuple)):
        return x
    t = getattr(x, "tensor", None)
    return t.name if t is not None else x.name


class Op:
    __slots__ = ("stream", "fn", "reads", "writes", "dma", "group", "deps", "inc", "ticket", "waits", "idx")

    def __init__(self, stream, fn, reads, writes, dma, group):
        self.stream = stream
        self.fn = fn
        self.reads = [_key(r) for r in reads]
        self.writes = [_key(w) for w in writes]
        self.dma = dma
        self.group = group
        self.deps = []
        self.inc = dma
        self.ticket = 0
        self.waits = []


class Prog:
    def __init__(self, nc):
        self.nc = nc
        self.ops = []
        self.stack = contextlib.ExitStack()

    def sb(self, name, shape, dtype):
        return self.stack.enter_context(self.nc.sbuf_tensor("s_" + name, list(shape), dtype))

    def ps(self, name, shape, dtype):
        return self.stack.enter_context(self.nc.psum_tensor("p_" + name, list(shape), dtype))

    def add(self, stream, fn, r=(), w=()):
        op = Op(stream, fn, r, w, False, None)
        self.ops.append(op)
        return op

    def dma(self, stream, out, in_, group=None, r=None, w=None, **kw):
        rr = [in_] if r is None else r
        ww = [out] if w is None else w
        g = group if group is not None else _key(ww[0])
        op = Op(stream, lambda e: e.dma_start(out=out, in_=in_, **kw), rr, ww, True, g)
        self.ops.append(op)
        return op

    def barrier_all(self, stream, keys):
        op = Op(stream, None, keys, [], False, None)
        self.ops.append(op)
        return op

    def build(self):
        nc = self.nc
        writers = {}
        readers = {}
        for i, op in enumerate(self.ops):
            op.idx = i
            deps = {}
            for k in op.reads:
                for d in writers.get(k, {}).values():
                    deps[d.idx] = (d, True)
            for k in op.writes:
                for d in writers.get(k, {}).values():
                    deps.setdefault(d.idx, (d, False))
                for d in readers.get(k, {}).values():
                    deps.setdefault(d.idx, (d, False))
            for d, raw in deps.values():
                if d is op:
                    continue
                if (not d.dma) and (not op.dma) and d.stream == op.stream:
                    if not (raw and op.stream in DEEP):
                        continue
                op.deps.append(d)
                d.inc = True
            wk = ("dma", op.group) if op.dma else op.stream
            for k in op.reads:
                readers.setdefault(k, {})[wk] = op
            for k in op.writes:
                writers.setdefault(k, {})[wk] = op
                readers[k] = {}
        cnt = {}
        for op in self.ops:
            if op.fn is None:
                continue
            if op.dma:
                sk = ("dma", op.group)
                cnt[sk] = cnt.get(sk, 0) + 16
                op.ticket = cnt[sk]
            elif op.inc:
                cnt[op.stream] = cnt.get(op.stream, 0) + 1
                op.ticket = cnt[op.stream]
        sems = {}
        for sk in cnt:
            sems[sk] = self.stack.enter_context(nc.semaphore("sm%d" % len(sems)))
        seen = {s: {} for s in STREAMS}
        for op in self.ops:
            need = {}
            for d in op.deps:
                sk = ("dma", d.group) if d.dma else d.stream
                need[sk] = max(need.get(sk, 0), d.ticket)
            for sk, v in need.items():
                if seen[op.stream].get(sk, 0) >= v:
                    continue
                seen[op.stream][sk] = v
                op.waits.append((sk, v))
        per = {s: [o for o in self.ops if o.stream == s] for s in STREAMS}
        self.nsem = len(sems)

        def runner(s):
            def f(e):
                for op in per[s]:
                    for sk, v in op.waits:
                        e.wait_ge(sems[sk], v)
                    if op.fn is None:
                        continue
                    ins = op.fn(e)
                    if op.dma:
                        ins.then_inc(sems[("dma", op.group)], 16)
                    elif op.inc:
                        ins.then_inc(sems[op.stream], 1)
            return f

        with nc.Block() as block:
            block.tensor(runner("tensor"))
            block.scalar(runner("scalar"))
            block.vector(runner("vector"))
            block.gpsimd(runner("gpsimd"))
            block.sync(runner("sync"))
        self.stack.close()


def mkap(t, offset, pat):
    return bass.AP(t, offset, [list(p) for p in pat])


def bcl(ap, n):
    return mkap(ap.tensor, ap.offset, [list(p) for p in ap.ap] + [[0, n]])


def weight_blocks():
    blks = []
    for pre in ("ff1", "ff2"):
        for hb in range(11):
            blks.append((pre + "_in%d" % hb, "w_" + pre + "_in", 1024, [(hb * 256, 256), (DFF + hb * 256, 256)]))
        for m in range(8):
            blks.append((pre + "_out%d" % m, "w_" + pre + "_out", DFF, [(m * 128, 128)]))
    blks.append(("in_q", "w_in", 1024, [(0, 512)]))
    blks.append(("in_kv", "w_in", 1024, [(512, 256)]))
    blks.append(("in_u", "w_in", 1024, [(768, 512)]))
    blks.append(("in_qm", "w_in", 1024, [(1280, 512)]))
    for i in range(6):
        blks.append(("in_g%d" % i, "w_in", 1024, [(1792 + 512 * i, 512)]))
    for i in range(4):
        blks.append(("glu%d" % i, "w_ssm_glu", 512, [(512 * i, 512)]))
    blks.append(("abr", "w_attn_br", 512, [(0, 1024)]))
    blks.append(("mbr", "w_mem_br", 512, [(0, 1024)]))
    for i in range(2):
        blks.append(("out%d" % i, "w_out", 1024, [(512 * i, 512)]))
    for i in range(2):
        blks.append(("mkv%d" % i, "w_mem_kv", 1024, [(512 * i, 512)]))
    return blks


WSHAPES = {"w_ff1_in": (1024, 2 * DFF), "w_ff1_out": (DFF, 1024), "w_ff2_in": (1024, 2 * DFF), "w_ff2_out": (DFF, 1024),
           "w_in": (1024, 4864), "w_ssm_glu": (512, 2048), "w_attn_br": (512, 1024), "w_mem_br": (512, 1024),
           "w_out": (1024, 1024), "w_mem_kv": (1024, 1024)}


def build_nc(nlight=NLIGHT, nseg=8, dbg=False):
    nc = bass.Bass("TRN2", target_bir_lowering=False)
    P = Prog(nc)

    def din(name, shape):
        return nc.dram_tensor(name, list(shape), F32, kind="ExternalInput").ap()

    def dout(name, shape):
        return nc.dram_tensor(name, list(shape), F32, kind="ExternalOutput").ap()

    xl = din("xl", [max(nlight, 1) * TT, D])
    xf = din("xf", [256 + nseg * TT, D])
    valid_d = din("valid", [128, 1])
    csk_d = din("csk", [4, 128, 128])
    csv_d = din("csv", [4, 128, 128])
    cmk_d = din("cmk", [4, 256, 512])
    cmv_d = din("cmv", [4, 256, 512])
    sre_d = din("sre", [128, 64])
    sim_d = din("sim", [128, 64])
    mem_d = din("mem", [256, D])
    biasP_d = din("biasP", [8, 128, 256])
    biasSc_d = din("biasSc", [8, 128, 32])
    biasSn_d = din("biasSn", [8, 128, 128])
    ident_d = din("ident", [128, 128])
    par_d = din("par", [128, 8])
    gains_d = {n: din(n, [D]) for n in ("ff1_pre_g", "ff1_post_g", "mix_pre_g", "mix_post_g", "mem_norm_g", "ff2_pre_g", "ff2_post_g")}
    sink_d = din("attn_sink", [8])
    lam_re_d = din("ssm_lambda_re", [32, 64])
    lam_im_d = din("ssm_lambda_im", [32, 64])
    logdt_d = din("ssm_log_dt", [32])
    bre_d = din("ssm_b_re", [32, 64, 16])
    bim_d = din("ssm_b_im", [32, 64, 16])
    cre_d = din("ssm_c_re", [32, 16, 64])
    cim_d = din("ssm_c_im", [32, 16, 64])
    dd_d = din("ssm_d", [512])
    wd = {n: din(n, s) for n, s in WSHAPES.items()}

    yf = dout("yf", [256 + nseg * TT, D])
    kp_o = dout("kp", [128, 128])
    vp_o = dout("vp", [128, 128])
    mkp_o = dout("mkp", [256, 512])
    mvp_o = dout("mvp", [256, 512])
    ssp_o = dout("ssp", [32, 128])
    ks_o = dout("ks", [128, 128])
    vs_o = dout("vs", [128, 128])
    sss_o = dout("sss", [128, 128])
    outs = ["yf", "kp", "vp", "mkp", "mvp", "ssp", "ks", "vs", "sss"]
    import os
    DBG = os.environ.get("KDBG") is not None
    if DBG:
        dbg_o = dout("dbg", [128, 16384])
        outs.append("dbg")
    dbgpos = [0]

    def dump(ap2d, n):
        if not DBG:
            return
        P.dma("gpsimd", dbg_o[:, dbgpos[0]:dbgpos[0] + n], ap2d, group="dbg")
        dbgpos[0] += n
        dbgpos[0] = (dbgpos[0] + 127) // 128 * 128

    blks = weight_blocks()
    scr = {}
    binfo = {}
    for (bn, wn, K, cols) in blks:
        kc = K // 128
        w = sum(c[1] for c in cols)
        scr[bn] = nc.dram_tensor("scr_" + bn, [128, kc * w], BF16, kind="Internal").ap()
        binfo[bn] = (kc, w)
        off = 0
        for (c0, cw) in cols:
            src = wd[wn][:, c0:c0 + cw].rearrange("(kc p) w -> p kc w", p=128)
            dst = scr[bn].rearrange("p (kc w) -> p kc w", kc=kc)[:, :, off:off + cw]
            P.dma("gpsimd", dst, src, group="cv_" + bn, w=[scr[bn]])
            off += cw

    NSLOT = 3
    wslots = [P.sb("wslot%d" % i, [128, 4096], BF16) for i in range(NSLOT)]
    wctr = [0]

    def wload(bn):
        kc, w = binfo[bn]
        s = wslots[wctr[0] % NSLOT]
        wctr[0] += 1
        P.dma("sync", s[:, 0:kc * w], scr[bn], group=s.name)
        return s[:, 0:kc * w].rearrange("p (kc w) -> p kc w", kc=kc)

    banks = [P.ps("bank%d" % i, [128, 512], F32) for i in range(4)]
    bctr = [0]

    def bank():
        b = banks[bctr[0] % 4]
        bctr[0] += 1
        return b
    pSb = [P.ps("pS%d" % i, [128, 512], F32) for i in range(2)]
    pWb = [P.ps("pW%d" % i, [128, 512], F32) for i in range(2)]
    banks8 = banks + pSb + pWb
    b8ctr = [0]

    def bank8():
        b = banks8[b8ctr[0] % 8]
        b8ctr[0] += 1
        return b

    xT = P.sb("xT", [128, 8, TT], F32)
    hT = P.sb("hT", [128, 8, TT], BF16)
    arena = P.sb("arena", [128, 22 * TT], BF16)
    hid = arena[:, :].rearrange("p (c t) -> p c t", t=TT)
    qT = arena[:, 0:4 * TT].rearrange("p (c t) -> p c t", t=TT)
    uT = arena[:, 4 * TT:8 * TT].rearrange("p (c t) -> p c t", t=TT)
    qmT = arena[:, 16 * TT:20 * TT].rearrange("p (c t) -> p c t", t=TT)
    yst = P.sb("yst", [128, 8, TT], F32)
    ytok = yst[:, :, :].rearrange("p c t -> p (c t)").rearrange("p (b f) -> p b f", f=D)
    sqb = [P.sb("sqb%d" % i, [128, TT], BF16) for i in range(2)]
    rstd = P.sb("rstd", [128, TT], F32)
    tmpf = P.sb("tmpf", [128, TT], F32)
    tmpf2 = P.sb("tmpf2", [128, TT], F32)
    tmpb = [P.sb("tmpb%d" % i, [128, TT], BF16) for i in range(2)]
    attnT = P.sb("attnT", [128, 4, TT], BF16)
    memT = P.sb("memT", [128, 4, TT], BF16)
    ysT = P.sb("ysT", [128, 4, TT], BF16)
    mergedT = P.sb("mergedT", [128, 8, TT], BF16)
    kTz = [[P.sb("kTz%d%d" % (i, j), [128, 128 + TT], BF16) for j in range(2)] for i in range(2)]
    vtok = P.sb("vtok", [128, 5, 128], BF16)
    kvf = P.sb("kvf", [128, 256], F32)
    Eb = [P.sb("Eb%d" % i, [128, 2, TT], BF16) for i in range(2)]
    ident = P.sb("ident", [128, 128], F32)
    identb = P.sb("identb", [128, 128], BF16)
    onesb = P.sb("onesb", [128, 128], BF16)
    onesv = P.sb("onesv", [128, 64], BF16)
    zer = P.sb("zer", [128, TT], BF16)
    valid = P.sb("validt", [128, 1], F32)
    par = P.sb("part", [128, 8], F32)
    gains = {n: P.sb("g_" + n, [128, 8], F32) for n in gains_d}
    hgains = {n: P.sb("hg_" + n, [128, 8], F32) for n in ("ff1_post_g", "ff2_post_g")}
    esink = P.sb("esink", [128, 4], F32)
    biasP = P.sb("biasP", [128, 8, 256], BF16)
    biasSc = P.sb("biasSc", [128, 8, 32], BF16)
    biasSn = P.sb("biasSn", [128, 8, 128], BF16)
    bstage = yst[:, 0:4, :].rearrange("p a t -> p (a t)").rearrange("p (h n) -> p h n", n=256)
    mkT = P.sb("mkT", [128, 4, 256], BF16)
    mvt = P.sb("mvt", [128, 2, 512], BF16)
    cstage = yst[:, 0:2, :]
    csb = P.sb("csb", [128, 2, 128], BF16)
    P_kpad_holder = [P.sb("kpad", [128, 2, 128], F32)]
    WS = yst[:, 6:8, :].rearrange("p a t -> p (a t)").rearrange("p (q v m) -> p q v m", q=4, v=2)
    WSg = P.sb("WSg", [128, 32, 2, 128], BF16)
    CWp = P.sb("CWp", [128, 2, 512], BF16)
    COS = P.sb("COS", [128, 32, 65], F32)
    SIN = P.sb("SIN", [128, 32, 65], F32)
    RT = P.sb("RT", [128, 32, 65], F32)
    Wb = P.sb("Wb", [128, 16, 65], F32)
    Wpb = P.sb("Wpb", [128, 16, 65], F32)
    t1 = P.sb("t1", [128, 16, 65], F32)
    t2 = P.sb("t2", [128, 16, 65], F32)
    sbf = P.sb("sbf", [128, 16, 65], BF16)
    Dcol = P.sb("Dcol", [128, 4], F32)
    car = P.sb("car", [128, 2, 32], F32)
    cars = P.sb("cars", [128, 2, 4, 32], F32)
    sm = P.sb("sm", [128, 16, 32], F32)
    wnT = [P.sb("wn%d" % i, [128, 16], F32) for i in range(2)]
    wswT = [P.sb("wsw%d" % i, [128, 16], F32) for i in range(2)]
    SINS = P.sb("SINS", [128, 32], F32)

    V = "vector"
    A = "scalar"
    G = "gpsimd"
    T = "tensor"

    def mm(out, lhsT, rhs, start=True, stop=True, tp=None):
        if tp is None:
            P.add(T, lambda e: e.matmul(out, lhsT=lhsT, rhs=rhs, start=start, stop=stop), r=[lhsT, rhs], w=[out])
        else:
            P.add(T, lambda e: e.matmul(out, lhsT=lhsT, rhs=rhs, start=start, stop=stop, tile_position=tp), r=[lhsT, rhs], w=[out])

    def tr(out, in_, idn):
        P.add(T, lambda e: e.transpose(out, in_, idn), r=[in_, idn], w=[out])

    def tt(out, a, b, op, eng=V):
        P.add(eng, lambda e: e.tensor_tensor(out, a, b, op), r=[a, b], w=[out])

    def ts(out, a, s1, op0, s2=None, op1=None, eng=V):
        rr = [a] + [s for s in (s1, s2) if not isinstance(s, (int, float, type(None)))]
        if op1 is None:
            P.add(eng, lambda e: e.tensor_scalar(out, a, s1, None, op0), r=rr, w=[out])
        else:
            P.add(eng, lambda e: e.tensor_scalar(out, a, s1, s2, op0, op1), r=rr, w=[out])

    def stt(out, a, s, b, op0, op1, eng=V):
        rr = [a, b] + ([] if isinstance(s, (int, float)) else [s])
        P.add(eng, lambda e: e.scalar_tensor_tensor(out, a, s, b, op0, op1), r=rr, w=[out])

    def cp(out, a, eng=V):
        P.add(eng, lambda e: e.tensor_copy(out, a), r=[a], w=[out])

    def act(out, a, func, scale=1.0, bias=None, eng=A):
        rr = [a] + ([] if bias is None or isinstance(bias, (int, float)) else [bias])
        if bias is None:
            P.add(eng, lambda e: e.activation(out=out, in_=a, func=func, scale=scale), r=rr, w=[out])
        else:
            P.add(eng, lambda e: e.activation(out=out, in_=a, func=func, scale=scale, bias=bias), r=rr, w=[out])

    def recip(out, a):
        P.add(V, lambda e: e.reciprocal(out, a), r=[a], w=[out])

    def mset(ap, v, eng=V):
        P.add(eng, lambda e: e.memset(ap, v), w=[ap])

    P.dma("sync", ident[:], ident_d)
    P.dma("sync", valid[:], valid_d)
    P.dma("sync", par[:], par_d)
    for n in gains_d:
        P.dma("sync", gains[n][:], gains_d[n].rearrange("(c p) -> p c", p=128), allow_slow_non_contiguous=True)
    for n in hgains:
        ts(hgains[n][:], gains[n][:], 0.5, ALU.mult)
    cp(identb[:], ident[:])
    mset(onesb[:], 1.0)
    mset(zer[:], 0.0)
    mset(tmpf[:, 0:64], 1.0)
    ts(onesv[:], tmpf[:, 0:64], valid[:, 0:1], ALU.mult)
    P.dma("sync", esink[0:64, :], mkap(sink_d.tensor, 0, [[0, 64], [2, 4]]), allow_slow_non_contiguous=True)
    P.dma("sync", esink[64:128, :], mkap(sink_d.tensor, 1, [[0, 64], [2, 4]]), allow_slow_non_contiguous=True)
    act(esink[:], esink[:], AF.Exp)
    P.dma("sync", bstage[:, :, :], biasP_d.rearrange("h k n -> k h n"))
    ts(biasP[:], bstage[:, :, :], 8.0, ALU.mult)
    P.dma("sync", bstage[:, :, 0:32], biasSc_d.rearrange("h k n -> k h n"))
    ts(biasSc[:], bstage[:, :, 0:32], 8.0, ALU.mult)
    P.dma("sync", bstage[:, :, 0:128], biasSn_d.rearrange("h k n -> k h n"))
    ts(biasSn[:], bstage[:, :, 0:128], 8.0, ALU.mult)
    for a in range(2):
        for b2 in range(2):
            mset(kTz[a][b2][:], 0.0)
    mset(P_kpad_holder[0][:], 0.0)

    lr = sm[:, 0, :]
    li = sm[:, 1, :]
    dt = sm[:, 2, :]
    for i, (dst, src) in enumerate(((lr, lam_re_d), (li, lam_im_d))):
        P.dma("sync", kvf[0:32, 0:64], src)
        P.dma("sync", kvf[0:32, 64:128], src)
        pb = bank()
        tr(pb[:, 0:32], kvf[0:32, 0:128], ident[0:32, 0:32])
        cp(dst, pb[:, 0:32])
    P.dma("sync", dt, mkap(logdt_d.tensor, 0, [[0, 128], [1, 32]]))
    act(dt, dt, AF.Exp)
    mag = sm[:, 3, :]
    cc = sm[:, 4, :]
    ss = sm[:, 5, :]
    q1 = sm[:, 6, :]
    q2 = sm[:, 7, :]
    are = sm[:, 8, :]
    aim = sm[:, 9, :]
    cre = sm[:, 10, :]
    cim = sm[:, 11, :]
    q3 = sm[:, 12, :]
    tt(q1, lr, dt, ALU.mult)
    act(mag, q1, AF.Exp)
    tt(q2, li, dt, ALU.mult)
    zz = sm[:, 13, :]
    z2 = sm[:, 14, :]
    vv = sm[:, 15, :]
    ts(zz, q2, 1.0 / 32, ALU.mult)
    tt(z2, zz, zz, ALU.mult)
    mset(q1, 1.0 / 362880)
    for cf in (-1.0 / 5040, 1.0 / 120, -1.0 / 6, 1.0):
        tt(q1, q1, z2, ALU.mult)
        ts(q1, q1, cf, ALU.add)
    tt(ss, q1, zz, ALU.mult)
    mset(q1, -1.0 / 3628800)
    for cf in (1.0 / 40320, -1.0 / 720, 1.0 / 24, -0.5):
        tt(q1, q1, z2, ALU.mult)
        ts(q1, q1, cf, ALU.add)
    tt(vv, q1, z2, ALU.mult)
    ts(vv, vv, -1.0, ALU.mult)
    for _ in range(5):
        ts(q1, vv, -1.0, ALU.mult, 1.0, ALU.add)
        tt(q3, ss, ss, ALU.mult)
        tt(ss, ss, q1, ALU.mult)
        ts(ss, ss, 2.0, ALU.mult)
        ts(vv, q3, 2.0, ALU.mult)
    ts(cc, vv, -1.0, ALU.mult, 1.0, ALU.add)
    tt(are, mag, cc, ALU.mult)
    tt(aim, mag, ss, ALU.mult)
    tt(q1, lr, lr, ALU.mult)
    tt(q3, li, li, ALU.mult)
    tt(q1, q1, q3, ALU.add)
    recip(q1, q1)
    ts(q3, are, -1.0, ALU.add)
    tt(cre, q3, lr, ALU.mult)
    tt(q2, aim, li, ALU.mult)
    tt(cre, cre, q2, ALU.add)
    tt(cre, cre, q1, ALU.mult)
    tt(cim, aim, lr, ALU.mult)
    tt(q2, q3, li, ALU.mult)
    tt(cim, cim, q2, ALU.subtract)
    tt(cim, cim, q1, ALU.mult)
    mset(COS[:], 1.0)
    mset(SIN[:], 0.0)
    cp(COS[:, :, 1], cc)
    cp(SIN[:, :, 1], ss)
    tA = Wb[:, :, :].rearrange("p g m -> p (g m)")[:, 0:1024].rearrange("p (g m) -> p g m", m=32)
    tB = Wpb[:, :, :].rearrange("p g m -> p (g m)")[:, 0:1024].rearrange("p (g m) -> p g m", m=32)
    for k in range(6):
        n = 1 << k
        c0 = bcl(COS[:, :, n], n)
        s0 = bcl(SIN[:, :, n], n)
        src_c = COS[:, :, 1:n + 1]
        src_s = SIN[:, :, 1:n + 1]
        tt(tA[:, :, 0:n], src_c, c0, ALU.mult)
        tt(tB[:, :, 0:n], src_s, s0, ALU.mult)
        tt(COS[:, :, n + 1:2 * n + 1], tA[:, :, 0:n], tB[:, :, 0:n], ALU.subtract)
        tt(tA[:, :, 0:n], src_c, s0, ALU.mult)
        tt(tB[:, :, 0:n], src_s, c0, ALU.mult)
        tt(SIN[:, :, n + 1:2 * n + 1], tA[:, :, 0:n], tB[:, :, 0:n], ALU.add)
    cp(RT[:], bcl(mag, 65))
    mset(RT[:, :, 0], 0.0)
    cp(SINS[:], SIN[:, :, 64])
    ts(SINS[64:128, :], SINS[64:128, :], -1.0, ALU.mult)
    BR = yst[:, 0, :].rearrange("p (g c) -> p g c", c=16)
    BI = yst[:, 1, :].rearrange("p (g c) -> p g c", c=16)
    XS = yst[:, 2, :].rearrange("p (g c) -> p g c", c=16)
    XW = yst[:, 3, :].rearrange("p (g c) -> p g c", c=16)
    for hf in range(2):
        P.dma("sync", BR[hf * 64:hf * 64 + 64], bre_d.rearrange("g p c -> p g c"))
        P.dma("sync", BI[hf * 64:hf * 64 + 64], bim_d.rearrange("g p c -> p g c"))
    crb = bcl(cre, 16)
    cib = bcl(cim, 16)
    tt(XS[:], BR[:], crb, ALU.mult)
    tt(XW[:], BI[:], cib, ALU.mult)
    tt(XS[:], XS[:], XW[:], ALU.subtract)
    tt(XW[:], BI[:], crb, ALU.mult)
    tt(BI[:], BR[:], cib, ALU.mult)
    tt(XW[:], XW[:], BI[:], ALU.add)
    cp(BR[:], XS[:])
    cp(XS[64:128], XW[64:128])
    ts(XW[64:128], BR[64:128], -1.0, ALU.mult)
    for q in range(4):
        for vi, X in enumerate((XS, XW)):
            pb = bank()
            tr(pb[:, 0:128], X[:, q * 8:(q + 1) * 8, :].rearrange("p g c -> p (g c)"), ident[:])
            cp(WS[:, q, vi, :], pb[:, 0:128])
            for g8 in range(8):
                ts(WSg[:, q * 8 + g8, vi, :], WS[:, q, vi, :], par[:, g8:g8 + 1], ALU.mult)
    CL = yst[:, 5, 0:128]
    CWf = yst[:, 4, :]
    for q in range(4):
        P.dma("sync", CL[:, 0:64], cre_d[q * 8:(q + 1) * 8].rearrange("g c p -> (g c) p"))
        P.dma("sync", CL[:, 64:128], cim_d[q * 8:(q + 1) * 8].rearrange("g c p -> (g c) p"))
        pb = bank()
        tr(pb[:, 0:128], CL[:], ident[:])
        cp(CWf[0:64, q * 128:(q + 1) * 128], pb[0:64, 0:128])
        ts(CWf[64:128, q * 128:(q + 1) * 128], pb[64:128, 0:128], -1.0, ALU.mult)
    mset(CWp[:], 0.0)
    c4 = CWf[:, :].rearrange("p (k t c) -> p k t c", t=2, c=16)
    for pr in range(2):
        o4 = CWp[:, pr, :].rearrange("p (k t c) -> p k t c", t=2, c=16)
        cp(o4[:, :, pr, :], c4[:, :, pr, :])
    P.dma("sync", Dcol[:], dd_d.rearrange("(q p) -> p q", p=128), allow_slow_non_contiguous=True)
    mset(car[:], 0.0)
    mset(attnT[:], 0.0)
    mset(memT[:], 0.0)
    mset(ysT[:], 0.0)
    mset(Wb[:], 0.0)
    mset(Wpb[:], 0.0)
    mset(t1[:], 0.0)
    mset(t2[:], 0.0)
    P.dma("sync", kvf[:, 0:64], sre_d)
    P.dma("sync", kvf[:, 64:128], sim_d)
    pb = bank()
    tr(pb[:, 0:128], kvf[:, 0:128], ident[:])
    cp(cars[:, 0, :, :].rearrange("p b g -> p (b g)"), pb[:, 0:128])
    cp(kvf[:, 128:192], kvf[:, 64:128])
    ts(kvf[:, 192:256], kvf[:, 0:64], -1.0, ALU.mult)
    pb = bank()
    tr(pb[:, 0:128], kvf[:, 128:256], ident[:])
    cp(cars[:, 1, :, :].rearrange("p b g -> p (b g)"), pb[:, 0:128])

    dump(sm[:, :, :].rearrange("p a b -> p (a b)"), 512)
    dump(COS[:, :, :].rearrange("p a b -> p (a b)"), 2080)
    dump(SIN[:, :, :].rearrange("p a b -> p (a b)"), 2080)
    dump(RT[:, :, :].rearrange("p a b -> p (a b)"), 2080)
    dump(cars[:, :, :, :].rearrange("p a b c -> p (a b c)"), 256)
    dump(Dcol[:, :], 4)
    dump(yst[:, 6:8, :].rearrange("p a t -> p (a t)"), 1024)
    dump(yst[:, 4, :], 512)
    STAGE = int(os.environ.get("KSTAGE", "99"))

    def finish():
        P.barrier_all("sync", outs)
        P.build()
        print("nsem", P.nsem, "nops", len(P.ops))
        return nc
    if STAGE == 0:
        return finish()

    def load_xT(src_rows, n):
        nb = n // 128
        P.dma("gpsimd", ytok[:, 0:nb, :], src_rows.rearrange("(b p) f -> p b f", p=128), group="xin")
        for b in range(nb):
            for h in range(2):
                pb = bank()
                for j in range(4):
                    tr(pb[:, j * 128:(j + 1) * 128], ytok[:, b, (h * 4 + j) * 128:(h * 4 + j + 1) * 128], ident[:])
                eng = V if (b + h) % 2 == 0 else A
                src = pb[:, :].rearrange("p (j t) -> p j t", t=128)
                if eng == V:
                    cp(xT[:, h * 4:h * 4 + 4, b * 128:(b + 1) * 128], src)
                else:
                    act(xT[:, h * 4:h * 4 + 4, b * 128:(b + 1) * 128], src, AF.Copy)

    def store_xT(dst_rows, n):
        nb = n // 128
        for b in range(nb):
            for h in range(2):
                pb = bank()
                for j in range(4):
                    tr(pb[:, j * 128:(j + 1) * 128], xT[:, h * 4 + j, b * 128:(b + 1) * 128], ident[:])
                if (b + h) % 2 == 0:
                    cp(ytok[:, b, h * 512:(h + 1) * 512], pb[:, :])
                else:
                    act(ytok[:, b, h * 512:(h + 1) * 512], pb[:, :], AF.Copy)
        P.dma("gpsimd", dst_rows.rearrange("(b p) f -> p b f", p=128), ytok[:, 0:nb, :], group="xout")

    def rstd_of(src, n, nch=8):
        pb = bank()
        for c in range(nch):
            s = sqb[c % 2]
            act(s[:, :n], src[:, c, :n], AF.Square)
            mm(pb[:, :n], onesb[:], s[:, :n], start=(c == 0), stop=(c == nch - 1))
        act(rstd[:, :n], pb[:, :n], AF.Sqrt, scale=1.0 / (128 * nch), bias=1e-6)
        recip(rstd[:, :n], rstd[:, :n])

    def prenorm(gn, n, dst=None, src=None):
        dst = hT if dst is None else dst
        src = xT if src is None else src
        rstd_of(src, n)
        for c in range(8):
            stt(dst[:, c, :n], src[:, c, :n], gains[gn][:, c:c + 1], rstd[:, :n], ALU.mult, ALU.mult)

    def postnorm_add(gn, n, half):
        g = hgains[gn] if half else gains[gn]
        rstd_of(yst, n)
        for c in range(8):
            tb_ = tmpf if c % 2 == 0 else tmpf2
            stt(tb_[:, :n], yst[:, c, :n], g[:, c:c + 1], rstd[:, :n], ALU.mult, ALU.mult)
            tt(xT[:, c, :n], xT[:, c, :n], tb_[:, :n], ALU.add, eng=(G if c % 2 else V))

    def ffn_gen(pre, n, deep=False):
        bk = bank8 if deep else bank
        for hb in range(11):
            wv = wload(pre + "_in%d" % hb)
            for h2 in range(2):
                pg = bk()
                pu = bk()
                for kc in range(8):
                    mm(pg[:, :n], wv[:, kc, h2 * 128:h2 * 128 + 128], hT[:, kc, :n], start=(kc == 0), stop=(kc == 7))
                for kc in range(8):
                    mm(pu[:, :n], wv[:, kc, 256 + h2 * 128:256 + h2 * 128 + 128], hT[:, kc, :n], start=(kc == 0), stop=(kc == 7))
                tb = tmpb[(hb * 2 + h2) % 2]
                act(tb[:, :n], pg[:, :n], AF.Silu)
                tt(hid[:, hb * 2 + h2, :n], tb[:, :n], pu[:, :n], ALU.mult)
            yield
        for m in range(8):
            wv = wload(pre + "_out%d" % m)
            po = bk()
            for kc in range(22):
                mm(po[:, :n], wv[:, kc, :], hid[:, kc, :n], start=(kc == 0), stop=(kc == 21))
            act(yst[:, m, :n], po[:, :n], AF.Copy)
            yield

    def ffn(pre, n, deep=False):
        for _ in ffn_gen(pre, n, deep):
            pass

    def u_proj(n, dst=None):
        dst = uT if dst is None else dst
        wv = wload("in_u")
        for m in range(4):
            pb = bank()
            for kc in range(8):
                mm(pb[:, :n], wv[:, kc, m * 128:(m + 1) * 128], hT[:, kc, :n], start=(kc == 0), stop=(kc == 7))
            act(dst[:, m, :n], pb[:, :n], AF.Copy)

    def ttk(out, a, b, op, eng, r, w):
        P.add(eng, lambda e: e.tensor_tensor(out, a, b, op), r=r, w=w)

    def cpk(out, a, eng, r, w):
        P.add(eng, lambda e: e.tensor_copy(out, a), r=r, w=w)

    pending = []

    def flush_pending():
        while pending:
            pending.pop(0)()

    def ssm_block(c0, n, cs, csw, full, usrc=None, halves=None):
        usrc = uT if usrc is None else usrc
        WbK = [("Wb", 0), ("Wb", 1), ("Wb", "c0")]
        WpK = [("Wp", 0), ("Wp", 1), ("Wp", "c0")]
        t1K = [("t1", 0), ("t1", 1)]
        t2K = [("t2", 0), ("t2", 1)]
        for hf in (range(2) if halves is None else halves):
            G0 = hf * 16
            pSv = [b[:, :].rearrange("p (g t) -> p g t", t=64) for b in pSb]
            pWv = [b[:, :].rearrange("p (g t) -> p g t", t=64) for b in pWb]
            for gi in range(16):
                g = G0 + gi
                q = g // 8
                rhs = usrc[:, q, c0:c0 + n]
                mm(pSv[gi // 8][:, gi % 8, :n], WSg[:, g, 0, :], rhs)
                mm(pWv[gi // 8][:, gi % 8, :n], WSg[:, g, 1, :], rhs)
            if (not full) and n == 64:
                cpk(Wb[:, :, 0], cs[:, G0:G0 + 16], G, [cs], [("Wb", "c0")])
                for hh in range(2):
                    gs = slice(hh * 8, hh * 8 + 8)
                    cs_ = COS[:, G0 + hh * 8:G0 + hh * 8 + 8, 1:n + 1]
                    sn_ = SIN[:, G0 + hh * 8:G0 + hh * 8 + 8, 1:n + 1]
                    ttk(Wb[:, gs, 1:n + 1], pSv[hh][:, :, :n], cs_, ALU.mult, V, [pSb[hh], COS], [("Wb", hh)])
                    ttk(t1[:, gs, 1:n + 1], pWv[hh][:, :, :n], sn_, ALU.mult, V, [pWb[hh], SIN], [("t1", hh)])
                    ttk(Wb[:, gs, 1:n + 1], Wb[:, gs, 1:n + 1], t1[:, gs, 1:n + 1], ALU.add, G, [("Wb", hh), ("t1", hh)], [("Wb", hh)])
                rflat = RT[:, G0:G0 + 16, :].rearrange("p g m -> p (g m)")
                sf = Wb[:, :, :].rearrange("p g m -> p (g m)")
                df = t1[:, :, :].rearrange("p g m -> p (g m)")
                P.add(V, lambda e, sf=sf, df=df, rflat=rflat: e.tensor_tensor_scan(out=df, data0=rflat, data1=sf, initial=0.0, op0=ALU.mult, op1=ALU.add),
                      r=[RT] + WbK, w=t1K)
                wn, wsw = wnT[hf], wswT[hf]
                flush_pending()
                cpk(wn[:, :], t1[:, :, n], G, t1K, [wn])
                P.dma("gpsimd", wsw[0:64, :], wn[64:128, :])
                P.dma("gpsimd", wsw[64:128, :], wn[0:64, :])

                def fin(wn=wn, wsw=wsw, G0=G0, cs=cs, n=n):
                    a1 = sm[:, 13, 0:16]
                    a2 = sm[:, 14, 0:16]
                    ttk(a1, wn[:, :], COS[:, G0:G0 + 16, n], ALU.mult, G, [wn, COS], [("a1",)])
                    ttk(a2, wsw[:, :], SINS[:, G0:G0 + 16], ALU.mult, G, [wsw, SINS], [("a2",)])
                    ttk(cs[:, G0:G0 + 16], a1, a2, ALU.subtract, G, [("a1",), ("a2",)], [cs])
                pending.append(fin)
                continue
            cpk(Wb[:, :, 0], cs[:, G0:G0 + 16], G, [cs], [("Wb", "c0")])
            cpk(Wpb[:, :, 0], csw[:, G0:G0 + 16], G, [csw], [("Wp", "c0")])
            for hh in range(2):
                gs = slice(hh * 8, hh * 8 + 8)
                cs_ = COS[:, G0 + hh * 8:G0 + hh * 8 + 8, 1:n + 1]
                sn_ = SIN[:, G0 + hh * 8:G0 + hh * 8 + 8, 1:n + 1]
                ttk(Wb[:, gs, 1:n + 1], pSv[hh][:, :, :n], cs_, ALU.mult, V, [pSb[hh], COS], [("Wb", hh)])
                ttk(t1[:, gs, 1:n + 1], pWv[hh][:, :, :n], sn_, ALU.mult, V, [pWb[hh], SIN], [("t1", hh)])
                ttk(Wb[:, gs, 1:n + 1], Wb[:, gs, 1:n + 1], t1[:, gs, 1:n + 1], ALU.add, G, [("Wb", hh), ("t1", hh)], [("Wb", hh)])
            for hh in range(2):
                gs = slice(hh * 8, hh * 8 + 8)
                cs_ = COS[:, G0 + hh * 8:G0 + hh * 8 + 8, 1:n + 1]
                sn_ = SIN[:, G0 + hh * 8:G0 + hh * 8 + 8, 1:n + 1]
                ttk(Wpb[:, gs, 1:n + 1], pWv[hh][:, :, :n], cs_, ALU.mult, V, [pWb[hh], COS], [("Wp", hh)])
                ttk(t2[:, gs, 1:n + 1], pSv[hh][:, :, :n], sn_, ALU.mult, V, [pSb[hh], SIN], [("t2", hh)])
                ttk(Wpb[:, gs, 1:n + 1], Wpb[:, gs, 1:n + 1], t2[:, gs, 1:n + 1], ALU.subtract, G, [("Wp", hh), ("t2", hh)], [("Wp", hh)])
            rflat = RT[:, G0:G0 + 16, :].rearrange("p g m -> p (g m)")
            for (src, dst, sk, dk) in ((Wb, t1, WbK, t1K), (Wpb, t2, WpK, t2K)):
                sf = src[:, :, :].rearrange("p g m -> p (g m)")
                df = dst[:, :, :].rearrange("p g m -> p (g m)")
                P.add(V, lambda e, sf=sf, df=df, rflat=rflat: e.tensor_tensor_scan(out=df, data0=rflat, data1=sf, initial=0.0, op0=ALU.mult, op1=ALU.add),
                      r=[RT] + sk, w=dk)
            cn = COS[:, G0:G0 + 16, n]
            snn = SIN[:, G0:G0 + 16, n]
            a1 = sm[:, 13, 0:16]
            a2 = sm[:, 14, 0:16]
            ttk(a1, t1[:, :, n], cn, ALU.mult, G, t1K + [COS], [("a1",)])
            ttk(a2, t2[:, :, n], snn, ALU.mult, G, t2K + [SIN], [("a2",)])
            ttk(cs[:, G0:G0 + 16], a1, a2, ALU.subtract, G, [("a1",), ("a2",)], [cs])
            ttk(a1, t2[:, :, n], cn, ALU.mult, G, t2K + [COS], [("a1",)])
            ttk(a2, t1[:, :, n], snn, ALU.mult, G, t1K + [SIN], [("a2",)])
            ttk(csw[:, G0:G0 + 16], a1, a2, ALU.add, G, [("a1",), ("a2",)], [csw])
            if not full:
                continue
            ca = COS[:, G0:G0 + 16, 0:n + 1]
            sa = SIN[:, G0:G0 + 16, 0:n + 1]
            ttk(Wb[:, :, 0:n + 1], t1[:, :, 0:n + 1], ca, ALU.mult, V, t1K + [COS], WbK)
            ttk(Wpb[:, :, 0:n + 1], t2[:, :, 0:n + 1], sa, ALU.mult, G, t2K + [SIN], WpK)
            ttk(sbf[:, :, 0:n + 1], Wb[:, :, 0:n + 1], Wpb[:, :, 0:n + 1], ALU.subtract, V, WbK + WpK, [sbf])
            py = bank()
            pyv = py[:, 0:128].rearrange("p (q t) -> p q t", t=64)
            for gi in range(16):
                g = G0 + gi
                q, w, pr = g // 8, (g % 8) // 2, g % 2
                mm(pyv[32 * w:32 * w + 32, q - 2 * hf, :n], CWp[:, pr, q * 128 + 32 * w:q * 128 + 32 * w + 32], sbf[:, gi, 1:n + 1],
                   start=(pr == 0), stop=(pr == 1), tp=(0, 32 * w))
            for ql in range(2):
                q = 2 * hf + ql
                stt(ysT[:, q, c0:c0 + n], usrc[:, q, c0:c0 + n], Dcol[:, q:q + 1], pyv[:, ql, :n], ALU.mult, ALU.add)

    def cross_attn(c0, n, mk, mv):
        for h in range(4):
            E = Eb[h % 2]
            for mb in range(2):
                pb = bank()
                mm(pb[:, :n], mk[:, h, mb * 128:(mb + 1) * 128], qmT[:, h, c0:c0 + n])
                act(E[:, mb, :n], pb[:, :n], AF.Exp, scale=128 ** -0.5)
            po = bank()
            pd = bank()
            for mb in range(2):
                mm(po[:, :n], mv[:, mb, h * 128:(h + 1) * 128], E[:, mb, :n], start=(mb == 0), stop=(mb == 1))
            for mb in range(2):
                mm(pd[:, :n], onesb[:], E[:, mb, :n], start=(mb == 0), stop=(mb == 1))
            recip(tmpf[:, :n], pd[:, :n])
            tt(memT[:, h, c0:c0 + n], po[:, :n], tmpf[:, :n], ALU.mult)

    def swa_unit(ps, kT_ap, q_ap, nq, bias_ap, v_ap, ones_ap, po_ap, pd_ap, E, poff):
        o = ps[:, 0:nq]
        mm(o, kT_ap, q_ap, start=True, stop=False)
        mm(o, identb[:, :], bias_ap, start=False, stop=True)
        e = E[:, 0, 0:nq]
        act(e, o, AF.Exp, scale=0.125)
        mm(po_ap, v_ap, e, start=False, stop=False, tp=(0, poff))
        mm(pd_ap, ones_ap, e, start=False, stop=False, tp=(0, poff))

    def swa_finish(hp, po, pd, c0, n):
        ts(tmpf[:, :n], pd[:, :n], esink[:, hp:hp + 1], ALU.add)
        recip(tmpf[:, :n], tmpf[:, :n])
        tt(attnT[:, hp, c0:c0 + n], po[:, :n], tmpf[:, :n], ALU.mult)

    def swa_prompt(n, first):
        nqc = n // 64
        for hp in range(4):
            po = banks[0]
            pd = banks[1]
            mm(po[:, :n], zer[:, 0:128], zer[:, :n])
            mm(pd[:, :n], zer[:, 0:128], zer[:, :n])
            kvh = hp // 2
            cnt = 0
            for kb in range(nqc // 2 + 1):
                qlo, qhi = max(2 * kb - 2, 0), min(2 * kb + 1, nqc - 1)
                nq = (qhi - qlo + 1) * 64
                r0 = qlo - (2 * kb - 2)
                for hh in range(2):
                    h = 2 * hp + hh
                    poff = hh * 64
                    ps = banks[2 + cnt % 2]
                    ones_ap = onesv[:, :] if (first and kb == 0) else onesb[:, 0:64]
                    swa_unit(ps, kTz[kvh][hh][:, kb * 128:(kb + 1) * 128],
                             qT[:, hp, qlo * 64:qlo * 64 + nq], nq,
                             biasP[:, h, r0 * 64:r0 * 64 + nq],
                             vtok[:, kb, kvh * 64:(kvh + 1) * 64], ones_ap,
                             po[poff:poff + 64, qlo * 64:qlo * 64 + nq], pd[poff:poff + 64, qlo * 64:qlo * 64 + nq], Eb[cnt % 2], poff)
                    cnt += 1
            swa_finish(hp, po, pd, 0, n)

    def kv_proj(n, c0k, vb0, want_out=None):
        wv = wload("in_kv")
        for kvh in range(2):
            pb = bank()
            for dup in range(2):
                for kc in range(8):
                    mm(pb[dup * 64:dup * 64 + 64, :n], wv[:, kc, kvh * 64:kvh * 64 + 64], hT[:, kc, :n], start=(kc == 0), stop=(kc == 7), tp=(0, dup * 64))
            act(kTz[kvh][0][0:64, c0k:c0k + n], pb[0:64, :n], AF.Copy)
            cp(kTz[kvh][1][64:128, c0k:c0k + n], pb[64:128, :n])
        for tb in range(n // 128):
            pb = bank()
            for kc in range(8):
                mm(pb[:, 0:256], hT[:, kc, tb * 128:(tb + 1) * 128], wv[:, kc, :], start=(kc == 0), stop=(kc == 7))
            cp(vtok[:, vb0 + tb, :], pb[:, 128:256])
            if want_out is not None and tb in want_out:
                ko, vo = want_out[tb]
                cp(kvf[:, :], pb[:, 0:256])
                P.dma("gpsimd", ko, kvf[:, 0:128], group="kvout")
                P.dma("gpsimd", vo, kvf[:, 128:256], group="kvout")

    def q_proj(n):
        for (bn, dst) in (("in_q", qT), ("in_qm", qmT)):
            wv = wload(bn)
            for m in range(4):
                pb = bank()
                for kc in range(8):
                    mm(pb[:, :n], wv[:, kc, m * 128:(m + 1) * 128], hT[:, kc, :n], start=(kc == 0), stop=(kc == 7))
                if m % 2 == 0:
                    act(dst[:, m, :n], pb[:, :n], AF.Copy)
                else:
                    cp(dst[:, m, :n], pb[:, :n])

    def merge_out(n):
        macc = yst
        wA = None
        for bi, br in enumerate((1, 0, 2)):
            if br == 0:
                wbr = wload("abr")
                src = attnT
            elif br == 2:
                wbr = wload("mbr")
                src = memT
            for jh in range(2):
                gw = wload("in_g%d" % (br * 2 + jh))
                if br == 1:
                    ga = wload("glu%d" % jh)
                    gb = wload("glu%d" % (2 + jh))
                for jj in range(4):
                    j = jh * 4 + jj
                    pg = bank()
                    for kc in range(8):
                        mm(pg[:, :n], gw[:, kc, jj * 128:(jj + 1) * 128], hT[:, kc, :n], start=(kc == 0), stop=(kc == 7))
                    sg = tmpb[0]
                    act(sg[:, :n], pg[:, :n], AF.Sigmoid)
                    if br == 1:
                        pa = bank()
                        pbk = bank()
                        for kc in range(4):
                            mm(pa[:, :n], ga[:, kc, jj * 128:(jj + 1) * 128], ysT[:, kc, :n], start=(kc == 0), stop=(kc == 3))
                        for kc in range(4):
                            mm(pbk[:, :n], gb[:, kc, jj * 128:(jj + 1) * 128], ysT[:, kc, :n], start=(kc == 0), stop=(kc == 3))
                        sb2 = tmpb[1]
                        act(sb2[:, :n], pbk[:, :n], AF.Sigmoid)
                        tt(tmpf[:, :n], pa[:, :n], sb2[:, :n], ALU.mult)
                        tt(macc[:, j, :n], tmpf[:, :n], sg[:, :n], ALU.mult)
                    else:
                        pa = bank()
                        for kc in range(4):
                            mm(pa[:, :n], wbr[:, kc, j * 128:(j + 1) * 128], src[:, kc, :n], start=(kc == 0), stop=(kc == 3))
                        tt(tmpf[:, :n], pa[:, :n], sg[:, :n], ALU.mult)
                        tt(macc[:, j, :n], macc[:, j, :n], tmpf[:, :n], ALU.add, eng=G)
        for c in range(8):
            act(mergedT[:, c, :n], macc[:, c, :n], AF.Copy)
        for i in range(2):
            wv = wload("out%d" % i)
            for mmi in range(4):
                m = i * 4 + mmi
                po = bank()
                for kc in range(8):
                    mm(po[:, :n], wv[:, kc, mmi * 128:(mmi + 1) * 128], mergedT[:, kc, :n], start=(kc == 0), stop=(kc == 7))
                act(yst[:, m, :n], po[:, :n], AF.Copy)

    memn = mergedT[:, :, 0:256]
    memf = xT
    P.dma("gpsimd", ytok[:, 0:2, :], mem_d.rearrange("(b p) f -> p b f", p=128), group="xin")
    for b in range(2):
        for h in range(2):
            pb = bank()
            for j in range(4):
                tr(pb[:, j * 128:(j + 1) * 128], ytok[:, b, (h * 4 + j) * 128:(h * 4 + j + 1) * 128], ident[:])
            cp(memf[:, h * 4:h * 4 + 4, b * 128:(b + 1) * 128], pb[:, :].rearrange("p (j t) -> p j t", t=128))
    prenorm("mem_norm_g", 256, dst=mergedT, src=memf)
    wk0 = wload("mkv0")
    wk1 = wload("mkv1")
    for h in range(4):
        pb = bank()
        for kc in range(8):
            mm(pb[:, 0:256], wk0[:, kc, h * 128:(h + 1) * 128], memn[:, kc, :], start=(kc == 0), stop=(kc == 7))
        cp(mkT[:, h, :], pb[:, 0:256])
    for mb in range(2):
        for wi, (wv, oo) in enumerate(((wk0, mkp_o), (wk1, mvp_o))):
            pb = bank()
            for kc in range(8):
                mm(pb[:, :], memn[:, kc, mb * 128:(mb + 1) * 128], wv[:, kc, :], start=(kc == 0), stop=(kc == 7))
            cp(cstage[:, wi, :], pb[:, :])
            if wi == 1:
                act(mvt[:, mb, :], pb[:, :], AF.Copy)
            P.dma("gpsimd", oo[mb * 128:(mb + 1) * 128, :], cstage[:, wi, :], group="memout")

    WbK_ = [("Wb", 0), ("Wb", 1), ("Wb", "c0")]
    t2K_ = [("t2", 0), ("t2", 1)]
    lstate = {"A": None, "G": None}

    def light_A(h, usrc):
        blk, hf = h
        c0, n, G0 = blk * 64, 64, hf * 16
        cs = car[:, 0, :]
        pSv = [b[:, :].rearrange("p (g t) -> p g t", t=64) for b in pSb]
        pWv = [b[:, :].rearrange("p (g t) -> p g t", t=64) for b in pWb]
        for gi in range(16):
            g = G0 + gi
            rhs = usrc[:, g // 8, c0:c0 + n]
            mm(pSv[gi // 8][:, gi % 8, :n], WSg[:, g, 0, :], rhs)
            mm(pWv[gi // 8][:, gi % 8, :n], WSg[:, g, 1, :], rhs)
        cpk(Wb[:, :, 0], cs[:, G0:G0 + 16], G, [cs], [("Wb", "c0")])
        for hh in range(2):
            gs = slice(hh * 8, hh * 8 + 8)
            cs_ = COS[:, G0 + hh * 8:G0 + hh * 8 + 8, 1:n + 1]
            sn_ = SIN[:, G0 + hh * 8:G0 + hh * 8 + 8, 1:n + 1]
            ttk(Wb[:, gs, 1:n + 1], pSv[hh][:, :, :n], cs_, ALU.mult, V, [pSb[hh], COS], [("Wb", hh)])
            ttk(t1[:, gs, 1:n + 1], pWv[hh][:, :, :n], sn_, ALU.mult, V, [pWb[hh], SIN], [("t1", hh)])
            ttk(Wb[:, gs, 1:n + 1], Wb[:, gs, 1:n + 1], t1[:, gs, 1:n + 1], ALU.add, G, [("Wb", hh), ("t1", hh)], [("Wb", hh)])

    def light_scan(h):
        blk, hf = h
        G0 = hf * 16
        rflat = RT[:, G0:G0 + 16, :].rearrange("p g m -> p (g m)")
        sf = Wb[:, :, :].rearrange("p g m -> p (g m)")
        df = t2[:, :, :].rearrange("p g m -> p (g m)")
        P.add(V, lambda e, sf=sf, df=df, rflat=rflat: e.tensor_tensor_scan(out=df, data0=rflat, data1=sf, initial=0.0, op0=ALU.mult, op1=ALU.add),
              r=[RT] + WbK_, w=t2K_)

    def light_gtail(h):
        blk, hf = h
        wn, wsw = wnT[hf], wswT[hf]
        cpk(wn[:, :], t2[:, :, 64], G, t2K_, [wn])
        P.dma("gpsimd", wsw[0:64, :], wn[64:128, :])
        P.dma("gpsimd", wsw[64:128, :], wn[0:64, :])

    def light_fin(h):
        blk, hf = h
        G0 = hf * 16
        wn, wsw = wnT[hf], wswT[hf]
        cs = car[:, 0, :]
        a1 = sm[:, 13, 0:16]
        a2 = sm[:, 14, 0:16]
        ttk(a1, wn[:, :], COS[:, G0:G0 + 16, 64], ALU.mult, G, [wn, COS], [("a1",)])
        ttk(a2, wsw[:, :], SINS[:, G0:G0 + 16], ALU.mult, G, [wsw, SINS], [("a2",)])
        ttk(cs[:, G0:G0 + 16], a1, a2, ALU.subtract, G, [("a1",), ("a2",)], [cs])

    def light_slot(h, usrc=None):
        pa, pg = lstate["A"], lstate["G"]
        if pa is not None:
            light_scan(pa)
        if pg is not None:
            light_fin(pg)
        if h is not None:
            light_A(h, usrc)
        if pa is not None:
            light_gtail(pa)
        lstate["G"] = pa
        lstate["A"] = h

    if STAGE == 1:
        return finish()
    uTl = mergedT[:, 0:4, :]
    for it in range(nlight):
        load_xT(xl[it * TT:(it + 1) * TT, :], TT)
        prenorm("ff1_pre_g", TT)
        gen = ffn_gen("ff1", TT)
        k = 0
        for _ in gen:
            if it > 0 and k < 16:
                light_slot((k // 2, k % 2), uTl)
            k += 1
        postnorm_add("ff1_post_g", TT, True)
        prenorm("mix_pre_g", TT)
        u_proj(TT, dst=uTl)
    if nlight > 0:
        for k in range(16):
            light_slot((k // 2, k % 2), uTl)
        light_slot(None)
        light_slot(None)
        P.dma("sync", car[0:64, 1, :], car[64:128, 0, :])
        P.dma("sync", car[64:128, 1, :], car[0:64, 0, :])
        ts(car[64:128, 1, :], car[64:128, 1, :], -1.0, ALU.mult)
    if STAGE == 2:
        return finish()
    def full_tile(ti):
        extra = (ti == 0)
        n = 256 if extra else TT
        r0 = 0 if extra else 256 + (ti - 1) * TT
        last = (ti == nseg)
        load_xT(xf[r0:r0 + n, :], n)
        prenorm("ff1_pre_g", n)
        ffn("ff1", n, deep=True)
        postnorm_add("ff1_post_g", n, True)
        prenorm("mix_pre_g", n)
        if extra:
            kv_proj(256, 0, 0, want_out={1: (ks_o, vs_o)})
            ts(vtok[:, 0, :], vtok[:, 0, :], valid[:, 0:1], ALU.mult)
        else:
            wo = {3: (kp_o, vp_o)} if last else None
            kv_proj(n, 128, 1, want_out=wo)
        q_proj(n)
        u_proj(n)
        if extra:
            for sq in range(4):
                c0 = 128 + 32 * sq
                ssm_block(c0, 32, cars[:, 0, sq, :], cars[:, 1, sq, :], True)
                KEX = int(os.environ.get("KEXTRA", "99"))
                if KEX == 1:
                    continue
                P.dma("gpsimd", cstage[:, :, :], cmk_d[sq].rearrange("(b p) f -> p b f", p=128), group="cst")
                mks = P_mks
                for mb in range(2):
                    pb = bank()
                    for h in range(4):
                        tr(pb[:, h * 128:(h + 1) * 128], cstage[:, mb, h * 128:(h + 1) * 128], ident[:])
                    cp(mks[:, :, mb * 128:(mb + 1) * 128], pb[:, :].rearrange("p (h t) -> p h t", t=128))
                P.dma("gpsimd", P_mvs[:, :, :], cmv_d[sq].rearrange("(b p) f -> p b f", p=128), group="cmv")
                cross_attn(c0, 32, mks, P_mvs)
                if KEX == 2:
                    continue
                P.dma("gpsimd", cstage[:, 0, 0:128], csk_d[sq], group="cst")
                P.dma("gpsimd", csb[:, 1, :], csv_d[sq], group="csb")
                kpad = P_kpad_holder[0]
                for kvh in range(2):
                    cp(kpad[:, 0, 0:64], cstage[:, 0, kvh * 64:(kvh + 1) * 64])
                    cp(kpad[:, 1, 64:128], cstage[:, 0, kvh * 64:(kvh + 1) * 64])
                    for hh in range(2):
                        pbt = bank()
                        tr(pbt[:, 0:128], kpad[:, hh, :], ident[:])
                        cp(P_ckT[kvh][hh][:, :], pbt[:, 0:128])
                for hp in range(4):
                    po = banks[0]
                    pd = banks[1]
                    mm(po[:, :32], zer[:, 0:128], zer[:, :32])
                    mm(pd[:, :32], zer[:, 0:128], zer[:, :32])
                    kvh = hp // 2
                    cnt = 0
                    for hh in range(2):
                        h = 2 * hp + hh
                        poff = hh * 64
                        qa = qT[:, hp, c0:c0 + 32]
                        for j in range(2):
                            ps = banks[2 + cnt % 2]
                            if j == 0:
                                kT_ap = P_ckT[kvh][hh][:, :]
                                v_ap = csb[:, 1, kvh * 64:(kvh + 1) * 64]
                                b_ap = biasSc[:, h, :]
                            else:
                                kT_ap = kTz[kvh][hh][:, 128:256]
                                v_ap = vtok[:, 1, kvh * 64:(kvh + 1) * 64]
                                b_ap = biasSn[:, h, 32 * sq:32 * sq + 32]
                            swa_unit(ps, kT_ap, qa, 32, b_ap, v_ap, onesb[:, 0:64],
                                     po[poff:poff + 64, 0:32], pd[poff:poff + 64, 0:32], Eb[cnt % 2], poff)
                            if DBG and sq == 0 and hp == 0:
                                cp(tmpf[:, 0:32], ps[:, 0:32])
                                dump(tmpf[:, 0:32], 32)
                            cnt += 1
                    if DBG and sq == 0 and hp == 0:
                        cp(tmpf[:, 0:32], po[:, 0:32])
                        dump(tmpf[:, 0:32], 32)
                        cp(tmpf[:, 0:32], pd[:, 0:32])
                        dump(tmpf[:, 0:32], 32)
                    swa_finish(hp, po, pd, c0, 32)
            for vi in range(1):
                pb = bank()
                tr(pb[:, 0:128], cars[:, 0, :, :].rearrange("p b g -> p (b g)"), ident[:])
                cp(kvf[:, 0:128], pb[:, 0:128])
                P.dma("gpsimd", sss_o, kvf[:, 0:128], group="kvout")
        else:
            for blk in range(8):
                ssm_block(blk * 64, 64, car[:, 0, :], car[:, 1, :], True)
            cross_attn(0, n, mkT, mvt)
            swa_prompt(n, first=(ti == 1))
            for kvh in range(2):
                for hh in range(2):
                    cp(kTz[kvh][hh][hh * 64:hh * 64 + 64, 0:128], kTz[kvh][hh][hh * 64:hh * 64 + 64, n:n + 128], eng=G)
            cp(vtok[:, 0, :], vtok[:, 4, :], eng=G)
            if last:
                pb = bank()
                tr(pb[0:32, 0:128], car[:, 0, :], ident[:])
                cp(kvf[0:32, 0:128], pb[0:32, 0:128])
                P.dma("gpsimd", ssp_o, kvf[0:32, 0:128], group="kvout")
        if extra:
            for buf in (uT, ysT, attnT, memT):
                for q in range(4):
                    cp(tmpf[:, 0:128], buf[:, q, 128:256])
                    dump(tmpf[:, 0:128], 128)
        merge_out(n)
        if extra:
            for c in range(8):
                dump(yst[:, c, 128:256], 128)
        postnorm_add("mix_post_g", n, False)
        prenorm("ff2_pre_g", n)
        ffn("ff2", n, deep=True)
        postnorm_add("ff2_post_g", n, True)
        store_xT(yf[r0:r0 + n, :], n)

    P_mks = arena[:, 20 * TT:22 * TT].rearrange("p (h m) -> p h m", m=256)
    P_mvs = mergedT[:, 0:2, :]
    P_ckT = [[P.sb("ckT%d%d" % (i, j), [128, 128], BF16) for j in range(2)] for i in range(2)]
    for ti in range(nseg + 1):
        if STAGE == 3 + ti:
            return finish()
        full_tile(ti)
    return finish()


def _t5_bucket(rel):
    half = 16
    max_exact = 8
    ret = (rel > 0).astype(np.int32) * half
    n = np.abs(rel)
    large = max_exact + (np.log(np.maximum(n, 1) / max_exact) / math.log(128 / max_exact) * (half - max_exact)).astype(np.int32)
    large = np.minimum(large, half - 1)
    return ret + np.where(n < max_exact, n, large)


_NC_CACHE = {}


def kernel(**inp):
    f = lambda a: np.ascontiguousarray(np.asarray(a, dtype=np.float32))
    xp = f(inp["x_prompt"])
    xs = f(inp["x_sample"])
    table = f(inp["rel_bias_table"])
    NEG = -1e30
    i = np.arange(64)[:, None]
    j = np.arange(192)[None, :]
    bp = table[_t5_bucket((j - 128) - i)]
    biasP = np.full((8, 128, 256), NEG, np.float32)
    for cb in range(2):
        for r in range(4):
            jb = cb + 2 - r
            if 0 <= jb <= 2:
                biasP[:, cb * 64:(cb + 1) * 64, r * 64:(r + 1) * 64] = bp[:, jb * 64:(jb + 1) * 64, :].transpose(2, 1, 0)
    i = np.arange(32)[:, None]
    j = np.arange(160)[None, :]
    bs = table[_t5_bucket((j - 128) - i)]
    bsT = bs.transpose(2, 1, 0)
    biasSc = np.ascontiguousarray(bsT[:, 0:128, :])
    biasSn = np.full((8, 128, 128), NEG, np.float32)
    for sq in range(4):
        biasSn[:, 32 * sq:32 * sq + 32, 32 * sq:32 * sq + 32] = bsT[:, 128:160, :]
    par = np.zeros((128, 8), np.float32)
    par[np.arange(128), np.arange(128) // 16] = 1.0
    shared = {"biasP": biasP, "biasSc": biasSc, "biasSn": biasSn, "ident": np.eye(128, dtype=np.float32), "par": par,
              "attn_sink": f(inp["attn_sink"]).reshape(8), "ssm_lambda_re": f(inp["ssm_lambda_re"])[0],
              "ssm_lambda_im": f(inp["ssm_lambda_im"])[0], "ssm_log_dt": f(inp["ssm_log_dt"]).reshape(32),
              "ssm_b_re": f(inp["ssm_b_re"])[0], "ssm_b_im": f(inp["ssm_b_im"])[0], "ssm_c_re": f(inp["ssm_c_re"])[0],
              "ssm_c_im": f(inp["ssm_c_im"])[0], "ssm_d": f(inp["ssm_d"]).reshape(512)}
    for n in ("ff1_pre_g", "ff1_post_g", "mix_pre_g", "mix_post_g", "mem_norm_g", "ff2_pre_g", "ff2_post_g"):
        shared[n] = f(inp[n]).reshape(D)
    for n in WSHAPES:
        shared[n] = f(inp[n])[0]
    in_maps = []
    SEQL = xp.shape[1]
    SEG = SEQL // 4
    nseg = SEG // TT
    nlight = 3 * nseg
    for c in range(NCORES):
        b, seg = c // 4, c % 4
        st = seg * SEG
        xl = np.zeros((nlight * TT, D), np.float32)
        if st > 0:
            xl[nlight * TT - st:] = xp[b, :st]
        xf = np.zeros((256 + SEG, D), np.float32)
        if st > 0:
            xf[0:128] = xp[b, st - 128:st]
        xf[128:256] = xs[4 * c:4 * c + 4].reshape(128, D)
        xf[256:] = xp[b, st:st + SEG]
        m = dict(shared)
        m.update({"xl": xl, "xf": xf, "valid": np.full((128, 1), 1.0 if seg > 0 else 0.0, np.float32),
                  "csk": f(inp["cache_swa_k"])[0, 4 * c:4 * c + 4].reshape(4, 128, 128),
                  "csv": f(inp["cache_swa_v"])[0, 4 * c:4 * c + 4].reshape(4, 128, 128),
                  "cmk": f(inp["cache_mem_k"])[0, 4 * c:4 * c + 4].reshape(4, 256, 512),
                  "cmv": f(inp["cache_mem_v"])[0, 4 * c:4 * c + 4].reshape(4, 256, 512),
                  "sre": f(inp["state_ssm_re"])[0, 4 * c:4 * c + 4].reshape(128, 64),
                  "sim": f(inp["state_ssm_im"])[0, 4 * c:4 * c + 4].reshape(128, 64),
                  "mem": f(inp["mem_prompt"])[b]})
        in_maps.append(m)
    key = (nlight, nseg)
    if key not in _NC_CACHE:
        _NC_CACHE[key] = build_nc(nlight, nseg)
    res = run_bass_kernel_spmd(_NC_CACHE[key], in_maps, core_ids=list(range(NCORES)))
    R = res.results
    _NC_CACHE["last"] = R
    yp = np.zeros((2, SEQL, D), np.float32)
    ys = np.zeros((32, 32, D), np.float32)
    for c in range(NCORES):
        b, seg = c // 4, c % 4
        yp[b, seg * SEG:(seg + 1) * SEG] = R[c]["yf"][256:]
        ys[4 * c:4 * c + 4] = R[c]["yf"][128:256].reshape(4, 32, D)
    kp = np.stack([R[4 * b + 3]["kp"].reshape(128, 2, 64) for b in range(2)])[None]
    vp = np.stack([R[4 * b + 3]["vp"].reshape(128, 2, 64) for b in range(2)])[None]
    mkp = np.stack([R[4 * b]["mkp"].reshape(256, 4, 128) for b in range(2)])[None]
    mvp = np.stack([R[4 * b]["mvp"].reshape(256, 4, 128) for b in range(2)])[None]
    srp = np.stack([R[4 * b + 3]["ssp"][:, 0:64] for b in range(2)])[None]
    sip = np.stack([R[4 * b + 3]["ssp"][:, 64:128] for b in range(2)])[None]
    ksn = np.concatenate([R[c]["ks"].reshape(4, 32, 2, 64) for c in range(NCORES)])[None]
    vsn = np.concatenate([R[c]["vs"].reshape(4, 32, 2, 64) for c in range(NCORES)])[None]
    srs = np.concatenate([R[c]["sss"][:, 0:64].reshape(4, 32, 64) for c in range(NCORES)])[None]
    sis = np.concatenate([R[c]["sss"][:, 64:128].reshape(4, 32, 64) for c in range(NCORES)])[None]
    return (yp, ys, np.ascontiguousarray(kp), np.ascontiguousarray(vp), np.ascontiguousarray(mkp), np.ascontiguousarray(mvp),
            np.ascontiguousarray(srp), np.ascontiguousarray(sip), ksn, vsn, srs, sis)
```
